# Optimizing a Trainium2 kernel written in Bass

```python
import math
import jax, jax.numpy as jnp
from jax import lax
import numpy as np

D_MODEL = 1024
BATCH = 8
SEQ = 4096
DEPTH = 2

N_MIXERS = 2
N_CONV_LAYERS = (DEPTH + 1) // 2
N_ATTN_LAYERS = DEPTH // 2
D_FF = 2816
CONV_WIDTH = 31
N_HEADS = 16
HEAD_DIM = D_MODEL // N_HEADS
D_ATTN = N_HEADS * HEAD_DIM
BLOCK_Q = 128
RMS_EPS = 1e-6
LN_EPS = 1e-5

kernel_name = "macaron_conv_stickbreaking_hybrid"


def rms_norm(x, g):
    xf = x.astype(jnp.float32)
    y = xf * lax.rsqrt(jnp.mean(xf * xf, axis=-1, keepdims=True) + RMS_EPS)
    return (y * g.astype(jnp.float32)).astype(x.dtype)


def layer_norm(x, g, b):
    xf = x.astype(jnp.float32)
    mu = jnp.mean(xf, axis=-1, keepdims=True)
    var = jnp.mean(jnp.square(xf - mu), axis=-1, keepdims=True)
    y = (xf - mu) * lax.rsqrt(var + LN_EPS)
    return (y * g.astype(jnp.float32) + b.astype(jnp.float32)).astype(x.dtype)


def swiglu_ffn(h, w_in, w_out):
    gate, up = jnp.split(h @ w_in, 2, axis=-1)
    return (jax.nn.silu(gate) * up) @ w_out


def conformer_conv(h, w_pw1, b_pw1, w_dw, b_dw, ln_g, ln_b, w_pw2, b_pw2):
    val, gate = jnp.split(h @ w_pw1 + b_pw1, 2, axis=-1)
    u = val * jax.nn.sigmoid(gate)
    u = lax.conv_general_dilated(
        u, w_dw[:, None, :].astype(u.dtype),
        window_strides=(1,),
        padding=((CONV_WIDTH - 1, 0),),
        dimension_numbers=("NWC", "WIO", "NWC"),
        feature_group_count=D_MODEL,
    ) + b_dw
    u = jax.nn.silu(layer_norm(u, ln_g, ln_b))
    return u @ w_pw2 + b_pw2


def stick_breaking_attention(h, w_qkv, w_o):
    b, s, _ = h.shape
    qkv = (h @ w_qkv).reshape(b, s, 3, N_HEADS, HEAD_DIM)
    qkv = jnp.transpose(qkv, (2, 0, 3, 1, 4))
    q, k, v = qkv[0], qkv[1], qkv[2]
    n_blk = s // BLOCK_Q
    q_blocks = jnp.transpose(q.reshape(b, N_HEADS, n_blk, BLOCK_Q, HEAD_DIM), (2, 0, 1, 3, 4))
    starts = jnp.arange(n_blk, dtype=jnp.int32) * BLOCK_Q
    key_pos = jnp.arange(s, dtype=jnp.int32)
    scale = 1.0 / math.sqrt(HEAD_DIM)

    def one_block(args):
        qb, start = args
        z = jnp.einsum("bhqd,bhkd->bhqk", qb, k).astype(jnp.float32) * scale
        q_pos = start + jnp.arange(BLOCK_Q, dtype=jnp.int32)
        mask = key_pos[None, :] < q_pos[:, None]
        log_beta = jax.nn.log_sigmoid(z)
        log_1m = jnp.where(mask, log_beta - z, 0.0)
        suffix = lax.cumsum(log_1m, axis=3, reverse=True) - log_1m
        a = jnp.where(mask, jnp.exp(log_beta + suffix), 0.0)
        return jnp.einsum("bhqk,bhkd->bhqd", a.astype(v.dtype), v)

    out = lax.map(one_block, (q_blocks, starts))
    out = jnp.transpose(out, (1, 0, 3, 2, 4)).reshape(b, s, D_ATTN)
    return out @ w_o


def setup_inputs(seed: int = 0) -> dict:
    key = jax.random.key(seed)
    ks = jax.random.split(key, 16)
    f32 = jnp.float32
    nrm = lambda k, shape, scale: jax.random.normal(k, shape, f32) * scale
    return {
        "x": jax.random.normal(ks[0], (BATCH, SEQ, D_MODEL), f32),
        "norm_g": 1.0 + nrm(ks[1], (DEPTH, 3, D_MODEL), 0.01),
        "final_g": 1.0 + nrm(ks[2], (D_MODEL,), 0.01),
        "ffn_w_in": nrm(ks[3], (DEPTH, 2, D_MODEL, 2 * D_FF), D_MODEL ** -0.5),
        "ffn_w_out": nrm(ks[4], (DEPTH, 2, D_FF, D_MODEL), D_FF ** -0.5),
        "conv_w_pw1": nrm(ks[5], (N_CONV_LAYERS, D_MODEL, 2 * D_MODEL), D_MODEL ** -0.5),
        "conv_b_pw1": nrm(ks[6], (N_CONV_LAYERS, 2 * D_MODEL), 0.01),
        "conv_w_dw": nrm(ks[7], (N_CONV_LAYERS, CONV_WIDTH, D_MODEL), CONV_WIDTH ** -0.5),
        "conv_b_dw": nrm(ks[8], (N_CONV_LAYERS, D_MODEL), 0.01),
        "conv_ln_g": 1.0 + nrm(ks[9], (N_CONV_LAYERS, D_MODEL), 0.01),
        "conv_ln_b": nrm(ks[10], (N_CONV_LAYERS, D_MODEL), 0.01),
        "conv_w_pw2": nrm(ks[11], (N_CONV_LAYERS, D_MODEL, D_MODEL), D_MODEL ** -0.5),
        "conv_b_pw2": nrm(ks[12], (N_CONV_LAYERS, D_MODEL), 0.01),
        "attn_w_qkv": nrm(ks[13], (N_ATTN_LAYERS, D_MODEL, 3 * D_ATTN), D_MODEL ** -0.5),
        "attn_w_o": nrm(ks[14], (N_ATTN_LAYERS, D_ATTN, D_MODEL), D_ATTN ** -0.5),
    }


def reference(x, norm_g, final_g, ffn_w_in, ffn_w_out,
              conv_w_pw1, conv_b_pw1, conv_w_dw, conv_b_dw, conv_ln_g, conv_ln_b,
              conv_w_pw2, conv_b_pw2, attn_w_qkv, attn_w_o):
    for i in range(DEPTH):
        g = norm_g[i]
        x = x + 0.5 * swiglu_ffn(rms_norm(x, g[0]), ffn_w_in[i, 0], ffn_w_out[i, 0])
        h = rms_norm(x, g[1])
        j = i // N_MIXERS
        if i % N_MIXERS == 0:
            x = x + conformer_conv(h, conv_w_pw1[j], conv_b_pw1[j], conv_w_dw[j], conv_b_dw[j],
                                   conv_ln_g[j], conv_ln_b[j], conv_w_pw2[j], conv_b_pw2[j])
        else:
            x = x + stick_breaking_attention(h, attn_w_qkv[j], attn_w_o[j])
        x = x + 0.5 * swiglu_ffn(rms_norm(x, g[2]), ffn_w_in[i, 1], ffn_w_out[i, 1])
    return rms_norm(x, final_g)
```

```python
import numpy as np
from contextlib import ExitStack
import concourse.bass as bass
import concourse.mybir as mybir
from concourse.bass_utils import run_bass_kernel_spmd

F32 = mybir.dt.float32
BF16 = mybir.dt.bfloat16
AF = mybir.ActivationFunctionType
ALU = mybir.AluOpType

D = 1024
S = 4096
T = 512
NT = S // T
DFF = 2816
NFC = DFF // 128
CW = 31
HALO = CW - 1
NCORES = 8
RMS_EPS = 1e-6
LN_EPS = 1e-5

PG = 0
PGF = 48
PB1V = 56
PB1G = 64
PBDW = 72
PLNG = 80
PLNB = 88
PB2 = 96
PWDW = 104
NPAR = PWDW + CW * 8

CI = 0
CONES = 128
CNEGU = 256
CMASK = 384
NCST = CMASK + 4 * 512

ENGS = ("pe", "act", "dve", "pool", "sp")


class Buf:
    __slots__ = ("w", "r")

    def __init__(self):
        self.w = None
        self.r = {}


class Op:
    __slots__ = ("eng", "fn", "deps", "sig", "seg", "val", "dma")


class Sched:
    def __init__(self):
        self.ops = {e: [] for e in ENGS}
        self.seg = 0
        self.dma_tot = {}

    def op(self, eng, fn, reads=(), writes=(), dma=None):
        o = Op()
        o.eng = eng
        o.fn = fn
        o.sig = False
        o.seg = self.seg
        o.val = None
        o.dma = None
        deps = []

        def add(d):
            if d is None or d in deps:
                return
            if d.dma is None and dma is None and d.eng == eng and eng == "pe":
                return
            if d.dma is None:
                d.sig = True
            deps.append(d)

        for b in reads:
            add(b.w)
        for b in writes:
            add(b.w)
            for r in b.r.values():
                add(r)
        o.deps = deps
        if dma is not None:
            tot = self.dma_tot.get(dma, 0) + 16
            self.dma_tot[dma] = tot
            o.dma = (dma, tot)
            key = ("dma", dma)
        else:
            key = eng
        for b in reads:
            b.r[key] = o
        for b in writes:
            b.w = o
            b.r = {}
        self.ops[eng].append(o)
        return o


class Slot:
    __slots__ = ("t", "buf", "name")

    def __init__(self, t, name=None):
        self.t = t
        self.buf = Buf()
        self.name = name


class Rot:
    def __init__(self, items):
        self.items = items
        self.i = 0

    def next(self):
        it = self.items[self.i % len(self.items)]
        self.i += 1
        return it


def build(nphases=99, ntiles=NT, debug=False):
    nc = bass.Bass("TRN2", target_bir_lowering=False)
    S_ = Sched()
    es = ExitStack()

    def dram(name, shape, dt, kind):
        return nc.dram_tensor(name, shape, dt, kind=kind).ap()

    xTd = dram("xT", [D, S], F32, "ExternalInput")
    pard = dram("par", [128, NPAR], F32, "ExternalInput")
    cstd = dram("cst", [128, NCST], F32, "ExternalInput")
    wspec = [("wi", 4 * 11 * 128, 4096), ("wo2", 4 * 8 * 128, 2816), ("wpw1", 4 * 128, 4096),
             ("wpw2", 2 * 128, 4096), ("wqkv", 6 * 128, 4096), ("wao", 2 * 128, 4096)]
    wf = {}
    wb = {}
    for name, rows, cols in wspec:
        wf[name] = dram(name, [rows, cols], F32, "ExternalInput")
        wb[name] = dram(name + "_bf", [rows, cols], BF16, "Internal")
    outd = dram("outT", [D, S], F32, "ExternalOutput")
    dbgd = dram("dbg", [3, 128, 8, T], F32, "ExternalOutput") if debug else None
    kcd = dram("kcache", [D, S], BF16, "Internal")
    vcd = dram("vcache", [S, D], BF16, "Internal")

    def sb(name, shape, dt):
        return es.enter_context(nc.sbuf_tensor(name, shape, dt))

    def pm(name):
        return es.enter_context(nc.psum_tensor(name, [128, 512], F32))

    xT = sb("xTs", [128, 8, T], F32)
    xTb = [Buf() for _ in range(8)]
    hT = sb("hT", [128, 8, T], BF16)
    hTb = [Buf() for _ in range(8)]
    G = sb("G", [128, 24, T], BF16)
    gT = G[:, 0:NFC, :]
    gTb = [Buf() for _ in range(NFC)]
    qT = G[:, 0:8, :]
    qTb = [Buf() for _ in range(8)]
    kst = G[:, 8:16, :]
    kstb = Buf()
    vst = G[:, 16:24, :].rearrange("p (a b) t -> p a (b t)", b=2)
    vstb = Buf()
    yT = sb("yT", [128, 8, T], F32)
    yTb = [Buf() for _ in range(8)]
    uT = sb("uT", [128, 8, HALO + T], BF16)
    uTb = [Buf() for _ in range(8)]
    ring = [Slot(sb(f"ring{i}", [128, 4096], BF16), f"ring{i}") for i in range(4)]
    ringrot = Rot(ring)
    kTp = [Slot(sb(f"kTp{i}", [128, S], BF16), f"kld{i}") for i in range(2)]
    vP = [Slot(sb(f"vP{i}", [128, S // 128, 128], BF16), f"vld{i}") for i in range(2)]
    f32pool = Rot([Slot(sb(f"f32p{i}", [128, 512], F32)) for i in range(8)])
    bfpool = Rot([Slot(sb(f"bfp{i}", [128, 512], BF16)) for i in range(6)])
    diag = Rot([Slot(sb(f"diag{i}", [128, CW, 128], BF16)) for i in range(2)])
    stat = [Slot(sb(f"stat{i}", [128, 512], F32)) for i in range(4)]
    carry = Rot([Slot(sb(f"carry{i}", [1, 2, 512], BF16)) for i in range(3)])
    par = sb("par_s", [128, NPAR], F32)
    parb = Buf()
    cst = sb("cst_s", [128, NCST], BF16)
    cstb = Buf()
    osT = Rot([Slot(sb(f"osT{i}", [128, 512], F32), f"out{i}") for i in range(4)])

    banks = [Slot(pm(f"ps{i}")) for i in range(8)]
    bankrot = Rot(banks[0:6])
    outrot = Rot(banks[6:8])

    ident = cst[:, CI:CI + 128]
    ones = cst[:, CONES:CONES + 128]
    negU = cst[:, CNEGU:CNEGU + 128]
    onesrow = cst[0:1, CONES:CONES + 128]

    def pcol(col):
        return par[:, col:col + 1]

    def MM(out, lhsT, rhs, start, stop, reads, writes):
        S_.op("pe", lambda e: e.matmul(out, lhsT=lhsT, rhs=rhs, start=start, stop=stop), reads, writes)

    def ACT(out, in_, func, reads, writes, bias=None, scale=None):
        kw = {}
        if bias is not None:
            kw["bias"] = bias
        if scale is not None:
            kw["scale"] = scale
        S_.op("act", lambda e: e.activation(out=out, in_=in_, func=func, **kw), reads, writes)

    def TT(eng, out, in0, in1, op, reads, writes):
        S_.op(eng, lambda e: e.tensor_tensor(out=out, in0=in0, in1=in1, op=op), reads, writes)

    def TS(eng, out, in0, s1, s2, op0, op1, reads, writes):
        if s2 is None:
            S_.op(eng, lambda e: e.tensor_scalar(out=out, in0=in0, scalar1=s1, scalar2=None, op0=op0), reads, writes)
        else:
            S_.op(eng, lambda e: e.tensor_scalar(out=out, in0=in0, scalar1=s1, scalar2=s2, op0=op0, op1=op1), reads, writes)

    def STT(eng, out, in0, scalar, in1, op0, op1, reads, writes):
        S_.op(eng, lambda e: e.scalar_tensor_tensor(out=out, in0=in0, scalar=scalar, in1=in1, op0=op0, op1=op1),
              reads, writes)

    def CP(eng, out, in_, reads, writes):
        if eng == "act":
            S_.op("act", lambda e: e.activation(out=out, in_=in_, func=AF.Copy), reads, writes)
        else:
            S_.op(eng, lambda e: e.tensor_copy(out=out, in_=in_), reads, writes)

    def RECIP(out, in_, reads, writes):
        S_.op("dve", lambda e: e.reciprocal(out=out, in_=in_), reads, writes)

    def DMA(eng, out, in_, sem, reads, writes):
        S_.op(eng, lambda e: e.dma_start(out=out, in_=in_), reads, writes, dma=sem)

    DMA("sp", par[:, :], pard[:, :], "par", [], [parb])
    DMA("pool", cst[:, :], cstd[:, :], "cst", [], [cstb])
    TS("dve", par[:, 0:PB1V], par[:, 0:PB1V], 32.0, None, ALU.mult, None, [parb], [parb])
    for c in range(8):
        S_.op("dve", (lambda c: lambda e: e.memset(uT[:, c, 0:HALO], 0.0))(c), [], [uTb[c]])

    wbuf = {}
    conv_order = [("wi", 0, 11), ("wo2", 0, 8), ("wpw1", 0, 4), ("wpw2", 0, 2), ("wi", 11, 11), ("wo2", 8, 8),
                  ("wi", 22, 11), ("wo2", 16, 8), ("wqkv", 0, 6), ("wao", 0, 2), ("wi", 33, 11), ("wo2", 24, 8)]
    ncv = 0
    for name, b0, nb in conv_order:
        b = b0
        while b < b0 + nb:
            n = min(4, b0 + nb - b)
            cb = Buf()
            DMA("pool", wb[name][b * 128:(b + n) * 128, :], wf[name][b * 128:(b + n) * 128, :], f"cv{ncv}", [], [cb])
            ncv += 1
            for bb in range(b, b + n):
                wbuf[(name, bb)] = cb
            b += n

    def wload(name, blk, ncols):
        slot = ringrot.next()
        DMA("sp", slot.t[:, 0:ncols], wb[name][blk * 128:(blk + 1) * 128, 0:ncols], slot.name,
            [wbuf[(name, blk)]], [slot.buf])
        return slot

    def rmsnorm(gcol, out_f32=None):
        ss = bankrot.next()
        for c in range(8):
            sq = bfpool.next()
            ACT(sq.t[:, :], xT[:, c, :], AF.Square, [xTb[c]], [sq.buf])
            MM(ss.t[:, :], ones, sq.t[:, :], c == 0, c == 7, [sq.buf, cstb], [ss.buf])
        rs = stat[0]
        ACT(rs.t[:, :], ss.t[:, :], AF.Sqrt, [ss.buf], [rs.buf], bias=D * RMS_EPS)
        RECIP(rs.t[:, :], rs.t[:, :], [rs.buf], [rs.buf])
        for c in range(8):
            if out_f32 is None:
                STT("dve", hT[:, c, :], xT[:, c, :], pcol(gcol + c), rs.t[:, :], ALU.mult, ALU.mult,
                    [xTb[c], rs.buf, parb], [hTb[c]])
            else:
                o = out_f32[c]
                STT("dve", o.t[:, :], xT[:, c, :], pcol(gcol + c), rs.t[:, :], ALU.mult, ALU.mult,
                    [xTb[c], rs.buf, parb], [o.buf])

    def ffn(lj):
        for blk in range(11):
            slot = wload("wi", lj * 11 + blk, 4096)
            for fc in range(2):
                f = blk * 2 + fc
                gp = bankrot.next()
                up = bankrot.next()
                for k in range(8):
                    MM(gp.t[:, :], slot.t[:, k * 512 + fc * 128:k * 512 + fc * 128 + 128], hT[:, k, :], k == 0, k == 7,
                       [slot.buf, hTb[k]], [gp.buf])
                for k in range(8):
                    MM(up.t[:, :], slot.t[:, k * 512 + 256 + fc * 128:k * 512 + 256 + fc * 128 + 128], hT[:, k, :],
                       k == 0, k == 7, [slot.buf, hTb[k]], [up.buf])
                sg = f32pool.next()
                ACT(sg.t[:, :], gp.t[:, :], AF.Silu, [gp.buf], [sg.buf])
                TT("dve", gT[:, f, :], sg.t[:, :], up.t[:, :], ALU.mult, [sg.buf, up.buf], [gTb[f]])
        for blk in range(8):
            slot = wload("wo2", lj * 8 + blk, 2816)
            op_ = bankrot.next()
            for k in range(NFC):
                MM(op_.t[:, :], slot.t[:, k * 128:(k + 1) * 128], gT[:, k, :], k == 0, k == NFC - 1,
                   [slot.buf, gTb[k]], [op_.buf])
            STT("dve", xT[:, blk, :], op_.t[:, :], 0.5, xT[:, blk, :], ALU.mult, ALU.add,
                [op_.buf, xTb[blk]], [xTb[blk]])

    def proj_fm(name, blk0, nblk, evac):
        for blk in range(nblk):
            slot = wload(name, blk0 + blk, 4096)
            for oc in range(4):
                p_ = bankrot.next()
                for k in range(8):
                    MM(p_.t[:, :], slot.t[:, k * 512 + oc * 128:k * 512 + oc * 128 + 128], hT[:, k, :], k == 0, k == 7,
                       [slot.buf, hTb[k]], [p_.buf])
                evac(blk * 4 + oc, p_)

    def conv_phase():
        rmsnorm(PG + 1 * 8)
        for blk in range(4):
            slot = wload("wpw1", blk, 4096)
            for fc in range(2):
                c = blk * 2 + fc
                vp = bankrot.next()
                gp = bankrot.next()
                for k in range(8):
                    MM(vp.t[:, :], slot.t[:, k * 512 + fc * 128:k * 512 + fc * 128 + 128], hT[:, k, :], k == 0, k == 7,
                       [slot.buf, hTb[k]], [vp.buf])
                for k in range(8):
                    MM(gp.t[:, :], slot.t[:, k * 512 + 256 + fc * 128:k * 512 + 256 + fc * 128 + 128], hT[:, k, :],
                       k == 0, k == 7, [slot.buf, hTb[k]], [gp.buf])
                sg = f32pool.next()
                ACT(sg.t[:, :], gp.t[:, :], AF.Sigmoid, [gp.buf, parb], [sg.buf], bias=pcol(PB1G + c))
                STT("dve", uT[:, c, HALO:HALO + T], vp.t[:, :], pcol(PB1V + c), sg.t[:, :], ALU.add, ALU.mult,
                    [vp.buf, sg.buf, parb], [uTb[c]])
        sumb = outrot.next()
        sqb = outrot.next()
        dgs = [None] * 8

        def build_diag(c):
            dg = diag.next()
            dgs[c] = dg
            for k in range(CW):
                TS("dve", dg.t[:, k, :], ident, pcol(PWDW + k * 8 + c), None, ALU.mult, None, [cstb, parb], [dg.buf])
        build_diag(0)
        for c in range(8):
            if c + 1 < 8:
                build_diag(c + 1)
            dg = dgs[c]
            yp = bankrot.next()
            for k in range(CW):
                MM(yp.t[:, :], dg.t[:, k, :], uT[:, c, k:k + T], k == 0, k == CW - 1, [dg.buf, uTb[c]], [yp.buf])
            CP("dve", uT[:, c, 0:HALO], uT[:, c, T:T + HALO], [uTb[c]], [uTb[c]])
            ACT(yT[:, c, :], yp.t[:, :], AF.Identity, [yp.buf, parb], [yTb[c]], bias=pcol(PBDW + c))
            ysq = bfpool.next()
            ACT(ysq.t[:, :], yp.t[:, :], AF.Square, [yp.buf, parb], [ysq.buf], bias=pcol(PBDW + c))
            ybf = bfpool.next()
            CP("dve", ybf.t[:, :], yT[:, c, :], [yTb[c]], [ybf.buf])
            MM(sumb.t[:, :], ones, ybf.t[:, :], c == 0, c == 7, [ybf.buf, cstb], [sumb.buf])
            MM(sqb.t[:, :], ones, ysq.t[:, :], c == 0, c == 7, [ysq.buf, cstb], [sqb.buf])
        mean, msq, var, rstd = stat[0], stat[1], stat[2], stat[3]
        TS("dve", mean.t[:, :], sumb.t[:, :], 1.0 / D, None, ALU.mult, None, [sumb.buf], [mean.buf])
        TT("dve", msq.t[:, :], mean.t[:, :], mean.t[:, :], ALU.mult, [mean.buf], [msq.buf])
        STT("dve", var.t[:, :], sqb.t[:, :], 1.0 / D, msq.t[:, :], ALU.mult, ALU.subtract, [sqb.buf, msq.buf], [var.buf])
        ACT(rstd.t[:, :], var.t[:, :], AF.Sqrt, [var.buf], [rstd.buf], bias=LN_EPS)
        RECIP(rstd.t[:, :], rstd.t[:, :], [rstd.buf], [rstd.buf])
        for c in range(8):
            v1 = f32pool.next()
            TT("dve", v1.t[:, :], yT[:, c, :], mean.t[:, :], ALU.subtract, [yTb[c], mean.buf], [v1.buf])
            STT("dve", v1.t[:, :], v1.t[:, :], pcol(PLNG + c), rstd.t[:, :], ALU.mult, ALU.mult,
                [v1.buf, rstd.buf, parb], [v1.buf])
            ACT(hT[:, c, :], v1.t[:, :], AF.Silu, [v1.buf, parb], [hTb[c]], bias=pcol(PLNB + c))

        def ev(c, p_):
            STT("dve", xT[:, c, :], p_.t[:, :], pcol(PB2 + c), xT[:, c, :], ALU.add, ALU.add,
                [p_.buf, xTb[c], parb], [xTb[c]])
        proj_fm("wpw2", 0, 2, ev)

    def att_phase(ti):
        rmsnorm(PG + (3 + 1) * 8)

        def evq(c, p_):
            ACT(qT[:, c, :], p_.t[:, :], AF.Identity, [p_.buf], [qTb[c]], scale=0.125)

        def evk(c, p_):
            CP("dve", kst[:, c, :], p_.t[:, :], [p_.buf], [kstb])
        proj_fm("wqkv", 0, 2, evq)
        proj_fm("wqkv", 2, 2, evk)
        for blk in range(2):
            slot = wload("wqkv", 4 + blk, 4096)
            for tc in range(4):
                p_ = bankrot.next()
                for k in range(8):
                    MM(p_.t[:, :], hT[:, k, tc * 128:(tc + 1) * 128], slot.t[:, k * 512:(k + 1) * 512], k == 0, k == 7,
                       [slot.buf, hTb[k]], [p_.buf])
                CP("act" if tc % 2 else "dve", vst[:, tc, blk * 512:(blk + 1) * 512], p_.t[:, :], [p_.buf], [vstb])
        kcb = Buf()
        vcb = Buf()
        DMA("pool", kcd.rearrange("(c p) s -> p c s", p=128)[:, :, ti * T:(ti + 1) * T], kst[:, :, :], "kst",
            [kstb], [kcb])
        DMA("pool", vcd.rearrange("(t p) f -> p t f", p=128)[:, ti * 4:(ti + 1) * 4, :], vst[:, :, :], "vst",
            [vstb], [vcb])
        nk = (ti + 1) * T
        nch = nk // 128

        def kvload(p):
            ks, vs = kTp[p % 2], vP[p % 2]
            DMA("pool", ks.t[:, 0:nk], kcd[p * 128:(p + 1) * 128, 0:nk], ks.name, [kcb], [ks.buf])
            DMA("pool", vs.t[:, 0:nch, :], vcd.rearrange("(t p) f -> p t f", p=128)[:, 0:nch, p * 128:(p + 1) * 128],
                vs.name, [vcb], [vs.buf])
        import os as _os
        attdbg = int(_os.environ.get("ATT_DEBUG", "0"))
        if attdbg != 1:
            kvload(0)
        attst = int(_os.environ.get("ATT_STAGE", "9"))
        for p in range(8 if attdbg == 0 else (1 if attdbg == 3 else 0)):
            if p + 1 < 8:
                kvload(p + 1)
            ks, vs = kTp[p % 2], vP[p % 2]
            for e in range(2):
                pr = slice(e * 64, (e + 1) * 64)
                ob = outrot.next()
                NB = nch
                st = [dict() for _ in range(NB)]

                def s_z(b):
                    ch = nch - 1 - b
                    zA = bankrot.next()
                    st[b]["zA"] = zA
                    diagb = b < 4
                    MM(zA.t[:, :], ks.t[pr, ch * 128:(ch + 1) * 128], qT[pr, p, :], True, not diagb,
                       [ks.buf, qTb[p]], [zA.buf])
                    if diagb:
                        d = 3 - b
                        MM(zA.t[:, :], ident, cst[:, CMASK + d * 512:CMASK + (d + 1) * 512], False, True,
                           [cstb], [zA.buf])

                def s_act1(b):
                    zA = st[b]["zA"]
                    eb_ = f32pool.next()
                    st[b]["e"] = eb_
                    ACT(eb_.t[:, :], zA.t[:, :], AF.Exp, [zA.buf], [eb_.buf])
                    sp = bfpool.next()
                    st[b]["sp"] = sp
                    ACT(sp.t[:, :], eb_.t[:, :], AF.Ln, [eb_.buf], [sp.buf], bias=1.0)

                def s_tri(b):
                    B = bankrot.next()
                    st[b]["B"] = B
                    sp = st[b]["sp"]
                    MM(B.t[:, :], negU, sp.t[:, :], True, b == 0 or attst < 3, [sp.buf, cstb], [B.buf])
                    if b > 0 and attst >= 3:
                        cr = st[b - 1]["cr"]
                        MM(B.t[:, :], onesrow, cr.t[0:1, 0, :], False, False, [cr.buf, cstb], [B.buf])
                        MM(B.t[:, :], onesrow, cr.t[0:1, 1, :], False, True, [cr.buf, cstb], [B.buf])

                def s_ext(b):
                    B = st[b]["B"]
                    if b < NB - 1:
                        cr = carry.next()
                        st[b]["cr"] = cr
                        CP("dve", cr.t[0:1, 0, :], B.t[0:1, :], [B.buf], [cr.buf])
                        TT("dve", cr.t[0:1, 1, :], B.t[0:1, :], cr.t[0:1, 0, :], ALU.subtract, [B.buf, cr.buf], [cr.buf])

                def s_act2(b):
                    B = st[b]["B"]
                    x2 = f32pool.next()
                    st[b]["eb"] = x2
                    extra = [st[b]["cr"].buf] if "cr" in st[b] else []
                    ACT(x2.t[:, :], B.t[:, :], AF.Exp, [B.buf] + extra, [x2.buf])

                def s_a(b):
                    a = bfpool.next()
                    st[b]["a"] = a
                    eb__ = st[b].get("eb", st[b]["e"])
                    TT("dve", a.t[:, :], st[b]["e"].t[:, :], eb__.t[:, :], ALU.mult,
                       [st[b]["e"].buf, eb__.buf], [a.buf])

                def s_av(b):
                    ch = nch - 1 - b
                    a = st[b]["a"]
                    MM(ob.t[:, :], vs.t[:, ch, :], a.t[:, :], b == 0, b == NB - 1, ([a.buf] if _os.environ.get('NOVDEP') else [vs.buf, a.buf]), [ob.buf])

                for it in range(NB + 3):
                    if it < NB:
                        s_z(it)
                        s_act1(it)
                    if 0 <= it - 1 < NB and attst >= 2:
                        s_tri(it - 1)
                        if attst >= 3:
                            s_ext(it - 1)
                        if attst >= 4:
                            if _os.environ.get("ATT_SKIP") != "act2":
                                s_act2(it - 1)
                            if _os.environ.get("ATT_SKIP") != "a":
                                s_a(it - 1)
                    if 0 <= it - 2 < NB and attst >= 5:
                        s_av(it - 2)
                if attst < 5:
                    continue
                CP("dve" if e == 0 else "act", hT[pr, p, :], ob.t[pr, :], [ob.buf], [hTb[p]])

        def evo(c, p_):
            TT("dve", xT[:, c, :], p_.t[:, :], xT[:, c, :], ALU.add, [p_.buf, xTb[c]], [xTb[c]])
        proj_fm("wao", 0, 2, evo)

    xTview = xTd.rearrange("(c p) s -> p c s", p=128)
    oTview = outd.rearrange("(c p) s -> p c s", p=128)
    outbufs = []
    for ti in range(ntiles):
        S_.seg = ti + 1
        cols = slice(ti * T, (ti + 1) * T)
        DMA("sp", xT[:, :, :], xTview[:, :, cols], "xld", [], xTb)
        ph = 0

        def more():
            nonlocal ph
            ph += 1
            return (ph in nphases) if isinstance(nphases, (set, frozenset)) else (ph <= nphases)
        if more():
            rmsnorm(PG + 0 * 8)
            ffn(0)
        if more():
            conv_phase()
            if debug and ti == 0:
                db = Buf()
                outbufs.append(db)
                DMA("pool", dbgd[0], uT[:, :, HALO:HALO + T], "dbg0", uTb, [db])
                DMA("pool", dbgd[1], yT[:, :, :], "dbg1", yTb, [db])
                DMA("pool", dbgd[2], hT[:, :, :], "dbg2", hTb, [db])
        if more():
            rmsnorm(PG + 2 * 8)
            ffn(1)
        if more():
            rmsnorm(PG + 3 * 8)
            ffn(2)
        if more():
            att_phase(ti)
        if more():
            rmsnorm(PG + 5 * 8)
            ffn(3)
        if more():
            ss = bankrot.next()
            for c in range(8):
                sq = bfpool.next()
                ACT(sq.t[:, :], xT[:, c, :], AF.Square, [xTb[c]], [sq.buf])
                MM(ss.t[:, :], ones, sq.t[:, :], c == 0, c == 7, [sq.buf, cstb], [ss.buf])
            rs = stat[0]
            ACT(rs.t[:, :], ss.t[:, :], AF.Sqrt, [ss.buf], [rs.buf], bias=D * RMS_EPS)
            RECIP(rs.t[:, :], rs.t[:, :], [rs.buf], [rs.buf])
            for c in range(8):
                o = osT.next()
                STT("dve", o.t[:, :], xT[:, c, :], pcol(PGF + c), rs.t[:, :], ALU.mult, ALU.mult,
                    [xTb[c], rs.buf, parb], [o.buf])
                ob_ = Buf()
                DMA("sp", outd[c * 128:(c + 1) * 128, cols], o.t[:, :], o.name, [o.buf], [ob_])
                outbufs.append(ob_)
        else:
            for c in range(8):
                ob_ = Buf()
                DMA("sp", outd[c * 128:(c + 1) * 128, cols], xT[:, c, :], f"xst{c}", [xTb[c]], [ob_])
                outbufs.append(ob_)
    S_.op("sp", None, outbufs, [])

    semnames = set()
    for eng in ENGS:
        cnt = {}
        for o in S_.ops[eng]:
            if o.dma is not None:
                semnames.add(("dma", o.dma[0]))
            elif o.sig:
                cnt[o.seg] = cnt.get(o.seg, 0) + 1
                o.val = cnt[o.seg]
                semnames.add((eng, o.seg))
    sems = {}
    for key in sorted(semnames, key=str):
        sems[key] = es.enter_context(nc.semaphore(f"s_{key[0]}_{key[1]}"))

    def event(d):
        if d.dma is not None:
            return sems[("dma", d.dma[0])], d.dma[1]
        return sems[(d.eng, d.seg)], d.val

    def emit(e, eng):
        waited = {}
        for o in S_.ops[eng]:
            for d in o.deps:
                sem, val = event(d)
                if waited.get(id(sem), 0) >= val:
                    continue
                e.wait_ge(sem, val)
                waited[id(sem)] = val
            if o.fn is None:
                continue
            ins = o.fn(e)
            if o.dma is not None:
                ins.then_inc(sems[("dma", o.dma[0])], 16)
            elif o.sig:
                ins.then_inc(sems[(eng, o.seg)], 1)

    with es:
        with nc.Block() as block:
            @block.tensor
            def _(e):
                emit(e, "pe")

            @block.scalar
            def _(e):
                emit(e, "act")

            @block.vector
            def _(e):
                emit(e, "dve")

            @block.gpsimd
            def _(e):
                emit(e, "pool")

            @block.sync
            def _(e):
                emit(e, "sp")
    return nc


def _tile_in(W, colsets):
    Wr = W.reshape(8, 128, W.shape[1])
    return np.concatenate(
        [np.ascontiguousarray(Wr[:, :, cols].transpose(1, 0, 2)).reshape(128, 8 * len(cols)) for cols in colsets], 0)


def _tile_out(W):
    Wr = W.reshape(NFC, 128, D)
    return np.concatenate(
        [np.ascontiguousarray(Wr[:, :, b * 128:(b + 1) * 128].transpose(1, 0, 2)).reshape(128, NFC * 128)
         for b in range(8)], 0)


def _consts():
    c = np.zeros((128, NCST), np.float32)
    c[:, CI:CI + 128] = np.eye(128, dtype=np.float32)
    c[:, CONES:CONES + 128] = 1.0
    j = np.arange(128)[:, None]
    s = np.arange(128)[None, :]
    c[:, CNEGU:CNEGU + 128] = np.where(j >= s, -1.0, 0.0)
    t = np.arange(512)[None, :]
    for d in range(4):
        c[:, CMASK + d * 512:CMASK + (d + 1) * 512] = np.where(128 * d + j >= t, -30000.0, 0.0)
    return c


def _prep(inputs):
    f = lambda a: np.asarray(a, dtype=np.float32)
    ffn_w_in, ffn_w_out = f(inputs["ffn_w_in"]), f(inputs["ffn_w_out"])
    r = np.arange(256)
    wi_sets = [np.concatenate([256 * b + r, DFF + 256 * b + r]) for b in range(11)]
    wi = np.concatenate([_tile_in(ffn_w_in[l, j], wi_sets) for l in range(2) for j in range(2)], 0)
    wo2 = np.concatenate([_tile_out(ffn_w_out[l, j]) for l in range(2) for j in range(2)], 0)
    pw1_sets = [np.concatenate([256 * b + r, D + 256 * b + r]) for b in range(4)]
    wpw1 = _tile_in(f(inputs["conv_w_pw1"])[0], pw1_sets)
    r5 = np.arange(512)
    wpw2 = _tile_in(f(inputs["conv_w_pw2"])[0], [512 * b + r5 for b in range(2)])
    wqkv = _tile_in(f(inputs["attn_w_qkv"])[0], [512 * b + r5 for b in range(6)])
    wao = _tile_in(f(inputs["attn_w_o"])[0], [512 * b + r5 for b in range(2)])
    rows = [f(inputs["norm_g"]).reshape(48, 128), f(inputs["final_g"]).reshape(8, 128),
            f(inputs["conv_b_pw1"]).reshape(16, 128), f(inputs["conv_b_dw"]).reshape(8, 128),
            f(inputs["conv_ln_g"]).reshape(8, 128), f(inputs["conv_ln_b"]).reshape(8, 128),
            f(inputs["conv_b_pw2"]).reshape(8, 128), f(inputs["conv_w_dw"]).reshape(CW * 8, 128)]
    par = np.ascontiguousarray(np.concatenate(rows, 0).T)
    assert par.shape == (128, NPAR)
    return dict(wi=wi, wo2=wo2, wpw1=wpw1, wpw2=wpw2, wqkv=wqkv, wao=wao, par=par, cst=_consts())


def kernel(**inputs):
    x = np.asarray(inputs["x"], dtype=np.float32)
    shared = _prep(inputs)
    nc = build()
    in_maps = []
    for b in range(NCORES):
        m = dict(shared)
        m["xT"] = np.ascontiguousarray(x[b].T)
        in_maps.append(m)
    res = run_bass_kernel_spmd(nc, in_maps, core_ids=list(range(NCORES)))
    out = np.stack([np.asarray(r["outT"]).T for r in res.results], 0)
    return np.ascontiguousarray(out.astype(np.float32))
```

```python
import numpy as np
from contextlib import ExitStack
import concourse.bass as bass
import concourse.mybir as mybir
from concourse.bass_utils import run_bass_kernel_spmd

F32 = mybir.dt.float32
BF16 = mybir.dt.bfloat16
AF = mybir.ActivationFunctionType
ALU = mybir.AluOpType

D = 1024
S = 4096
T = 512
NT = S // T
DFF = 2816
NFC = DFF // 128
CW = 31
HALO = CW - 1
NCORES = 8
RMS_EPS = 1e-6
LN_EPS = 1e-5

PG = 0
PGF = 48
PB1V = 56
PB1G = 64
PBDW = 72
PLNG = 80
PLNB = 88
PB2 = 96
PWDW = 104
NPAR = PWDW + CW * 8

CI = 0
CONES = 128
CNEGU = 256
CMASK = 384
NCST = CMASK + 4 * 512

ENGS = ("pe", "act", "dve", "pool", "sp")
import os as _os0
SAME_ENGINE_EDGES = _os0.environ.get("SEE", "1") == "1"


class Buf:
    __slots__ = ("w", "r")

    def __init__(self):
        self.w = None
        self.r = {}


class Op:
    __slots__ = ("eng", "fn", "deps", "sig", "seg", "val", "dma")


class Sched:
    def __init__(self):
        self.ops = {e: [] for e in ENGS}
        self.seg = 0
        self.dma_tot = {}

    def op(self, eng, fn, reads=(), writes=(), dma=None):
        o = Op()
        o.eng = eng
        o.fn = fn
        o.sig = False
        o.seg = self.seg
        o.val = None
        o.dma = None
        deps = []

        def add(d):
            if d is None or d in deps:
                return
            if d.dma is None and dma is None and d.eng == eng and (eng == "pe" or not SAME_ENGINE_EDGES):
                return
            if d.dma is None:
                d.sig = True
            deps.append(d)

        for b in reads:
            add(b.w)
        for b in writes:
            add(b.w)
            for r in b.r.values():
                add(r)
        o.deps = deps
        if dma is not None:
            tot = self.dma_tot.get(dma, 0) + 16
            self.dma_tot[dma] = tot
            o.dma = (dma, tot)
            key = ("dma", dma)
        else:
            key = eng
        for b in reads:
            b.r[key] = o
        for b in writes:
            b.w = o
            b.r = {}
        self.ops[eng].append(o)
        return o


class Slot:
    __slots__ = ("t", "buf", "name")

    def __init__(self, t, name=None):
        self.t = t
        self.buf = Buf()
        self.name = name


class Rot:
    def __init__(self, items):
        self.items = items
        self.i = 0

    def next(self):
        it = self.items[self.i % len(self.items)]
        self.i += 1
        return it


def build(nphases=99, ntiles=NT, debug=False):
    nc = bass.Bass("TRN2", target_bir_lowering=False)
    S_ = Sched()
    es = ExitStack()

    def dram(name, shape, dt, kind):
        return nc.dram_tensor(name, shape, dt, kind=kind).ap()

    xTd = dram("xT", [D, S], F32, "ExternalInput")
    pard = dram("par", [128, NPAR], F32, "ExternalInput")
    cstd = dram("cst", [128, NCST], F32, "ExternalInput")
    wspec = [("wi", 4 * 11 * 128, 4096), ("wo2", 4 * 8 * 128, 2816), ("wpw1", 4 * 128, 4096),
             ("wpw2", 2 * 128, 4096), ("wqkv", 6 * 128, 4096), ("wao", 2 * 128, 4096)]
    wf = {}
    wb = {}
    for name, rows, cols in wspec:
        wf[name] = dram(name, [rows, cols], F32, "ExternalInput")
        wb[name] = dram(name + "_bf", [rows, cols], BF16, "Internal")
    outd = dram("outT", [D, S], F32, "ExternalOutput")
    dbgd = dram("dbg", [3, 128, 8, T], F32, "ExternalOutput") if debug else None
    kcd = dram("kcache", [D, S], BF16, "Internal")
    vcd = dram("vcache", [S, D], BF16, "Internal")

    def sb(name, shape, dt):
        return es.enter_context(nc.sbuf_tensor(name, shape, dt))

    def pm(name):
        return es.enter_context(nc.psum_tensor(name, [128, 512], F32))

    xT = sb("xTs", [128, 8, T], F32)
    xTb = [Buf() for _ in range(8)]
    hT = sb("hT", [128, 8, T], BF16)
    hTb = [Buf() for _ in range(8)]
    G = sb("G", [128, 24, T], BF16)
    gT = G[:, 0:NFC, :]
    gTb = [Buf() for _ in range(NFC)]
    qT = G[:, 0:8, :]
    qTb = [Buf() for _ in range(8)]
    kst = G[:, 8:16, :]
    kstb = Buf()
    vst = G[:, 16:24, :].rearrange("p (a b) t -> p a (b t)", b=2)
    vstb = Buf()
    yT = sb("yT", [128, 8, T], F32)
    yTb = [Buf() for _ in range(8)]
    uT = sb("uT", [128, 8, HALO + T], BF16)
    uTb = [Buf() for _ in range(8)]
    ring = [Slot(sb(f"ring{i}", [128, 4096], BF16), f"ring{i}") for i in range(4)]
    ringrot = Rot(ring)
    kTp = [Slot(sb(f"kTp{i}", [128, S], BF16), f"kld{i}") for i in range(2)]
    vP = [Slot(sb(f"vP{i}", [128, S // 128, 128], BF16), f"vld{i}") for i in range(2)]
    f32pool = Rot([Slot(sb(f"f32p{i}", [128, 512], F32)) for i in range(8)])
    bfpool = Rot([Slot(sb(f"bfp{i}", [128, 512], BF16)) for i in range(8)])
    diag = Rot([Slot(sb(f"diag{i}", [128, CW, 128], BF16)) for i in range(2)])
    stat = [Slot(sb(f"stat{i}", [128, 512], F32)) for i in range(4)]
    carry = Rot([Slot(sb(f"carry{i}", [1, 512], BF16)) for i in range(4)])
    ones32 = sb("ones32", [1, 128], F32)
    o32b = Buf()
    par = sb("par_s", [128, NPAR], F32)
    parb = Buf()
    cst = sb("cst_s", [128, NCST], BF16)
    cstb = Buf()
    osT = Rot([Slot(sb(f"osT{i}", [128, 512], F32), f"out{i}") for i in range(2)])
    epool = Rot([Slot(yT[:, i, :]) for i in range(6)])

    banks = [Slot(pm(f"ps{i}")) for i in range(8)]
    bankrot = Rot(banks[0:6])
    outrot = Rot(banks[6:8])

    ident = cst[:, CI:CI + 128]
    ones = cst[:, CONES:CONES + 128]
    negU = cst[:, CNEGU:CNEGU + 128]
    onesrow = cst[0:1, CONES:CONES + 128]

    def pcol(col):
        return par[:, col:col + 1]

    def MM(out, lhsT, rhs, start, stop, reads, writes):
        S_.op("pe", lambda e: e.matmul(out, lhsT=lhsT, rhs=rhs, start=start, stop=stop), reads, writes)

    def ACT(out, in_, func, reads, writes, bias=None, scale=None):
        kw = {}
        if bias is not None:
            kw["bias"] = bias
        if scale is not None:
            kw["scale"] = scale
        S_.op("act", lambda e: e.activation(out=out, in_=in_, func=func, **kw), reads, writes)

    def TT(eng, out, in0, in1, op, reads, writes):
        S_.op(eng, lambda e: e.tensor_tensor(out=out, in0=in0, in1=in1, op=op), reads, writes)

    def TS(eng, out, in0, s1, s2, op0, op1, reads, writes):
        if s2 is None:
            S_.op(eng, lambda e: e.tensor_scalar(out=out, in0=in0, scalar1=s1, scalar2=None, op0=op0), reads, writes)
        else:
            S_.op(eng, lambda e: e.tensor_scalar(out=out, in0=in0, scalar1=s1, scalar2=s2, op0=op0, op1=op1), reads, writes)

    def STT(eng, out, in0, scalar, in1, op0, op1, reads, writes):
        S_.op(eng, lambda e: e.scalar_tensor_tensor(out=out, in0=in0, scalar=scalar, in1=in1, op0=op0, op1=op1),
              reads, writes)

    def CP(eng, out, in_, reads, writes):
        if eng == "act":
            S_.op("act", lambda e: e.activation(out=out, in_=in_, func=AF.Copy), reads, writes)
        else:
            S_.op(eng, lambda e: e.tensor_copy(out=out, in_=in_), reads, writes)

    def RECIP(out, in_, reads, writes):
        S_.op("dve", lambda e: e.reciprocal(out=out, in_=in_), reads, writes)

    def DMA(eng, out, in_, sem, reads, writes):
        S_.op(eng, lambda e: e.dma_start(out=out, in_=in_), reads, writes, dma=sem)

    DMA("sp", par[:, :], pard[:, :], "par", [], [parb])
    DMA("pool", cst[:, :], cstd[:, :], "cst", [], [cstb])
    TS("dve", par[:, 0:PB1V], par[:, 0:PB1V], 32.0, None, ALU.mult, None, [parb], [parb])
    for c in range(8):
        S_.op("dve", (lambda c: lambda e: e.memset(uT[:, c, 0:HALO], 0.0))(c), [], [uTb[c]])
    S_.op("dve", lambda e: e.memset(ones32[0:1, :], 1.0), [], [o32b])

    wbuf = {}
    conv_order = [("wi", 0, 11), ("wo2", 0, 8), ("wpw1", 0, 4), ("wpw2", 0, 2), ("wi", 11, 11), ("wo2", 8, 8),
                  ("wi", 22, 11), ("wo2", 16, 8), ("wqkv", 0, 6), ("wao", 0, 2), ("wi", 33, 11), ("wo2", 24, 8)]
    ncv = 0
    for name, b0, nb in conv_order:
        b = b0
        while b < b0 + nb:
            n = min(4, b0 + nb - b)
            cb = Buf()
            DMA("pool", wb[name][b * 128:(b + n) * 128, :], wf[name][b * 128:(b + n) * 128, :], f"cv{ncv}", [], [cb])
            ncv += 1
            for bb in range(b, b + n):
                wbuf[(name, bb)] = cb
            b += n

    def wload(name, blk, ncols):
        slot = ringrot.next()
        DMA("sp", slot.t[:, 0:ncols], wb[name][blk * 128:(blk + 1) * 128, 0:ncols], slot.name,
            [wbuf[(name, blk)]], [slot.buf])
        return slot

    def rmsnorm(gcol, out_f32=None):
        ss = bankrot.next()
        for c in range(8):
            sq = bfpool.next()
            ACT(sq.t[:, :], xT[:, c, :], AF.Square, [xTb[c]], [sq.buf])
            MM(ss.t[:, :], ones, sq.t[:, :], c == 0, c == 7, [sq.buf, cstb], [ss.buf])
        rs = stat[0]
        ACT(rs.t[:, :], ss.t[:, :], AF.Sqrt, [ss.buf], [rs.buf], bias=D * RMS_EPS)
        RECIP(rs.t[:, :], rs.t[:, :], [rs.buf], [rs.buf])
        for c in range(8):
            if False:
                tmp = f32pool.next()
                TS("pool", tmp.t[:, :], xT[:, c, :], pcol(gcol + c), None, ALU.mult, None, [xTb[c], parb], [tmp.buf])
                TT("pool", hT[:, c, :], tmp.t[:, :], rs.t[:, :], ALU.mult, [tmp.buf, rs.buf], [hTb[c]])
            elif out_f32 is None:
                STT("dve", hT[:, c, :], xT[:, c, :], pcol(gcol + c), rs.t[:, :],
                    ALU.mult, ALU.mult, [xTb[c], rs.buf, parb], [hTb[c]])
            else:
                o = out_f32[c]
                STT("dve", o.t[:, :], xT[:, c, :], pcol(gcol + c), rs.t[:, :], ALU.mult, ALU.mult,
                    [xTb[c], rs.buf, parb], [o.buf])

    def ffn(lj):
        for blk in range(11):
            slot = wload("wi", lj * 11 + blk, 4096)
            for fc in range(2):
                f = blk * 2 + fc
                gp = bankrot.next()
                up = bankrot.next()
                for k in range(8):
                    MM(gp.t[:, :], slot.t[:, k * 512 + fc * 128:k * 512 + fc * 128 + 128], hT[:, k, :], k == 0, k == 7,
                       [slot.buf, hTb[k]], [gp.buf])
                for k in range(8):
                    MM(up.t[:, :], slot.t[:, k * 512 + 256 + fc * 128:k * 512 + 256 + fc * 128 + 128], hT[:, k, :],
                       k == 0, k == 7, [slot.buf, hTb[k]], [up.buf])
                sg = f32pool.next()
                ACT(sg.t[:, :], gp.t[:, :], AF.Silu, [gp.buf], [sg.buf])
                TT("dve", gT[:, f, :], sg.t[:, :], up.t[:, :], ALU.mult, [sg.buf, up.buf], [gTb[f]])
        for blk in range(8):
            slot = wload("wo2", lj * 8 + blk, 2816)
            op_ = bankrot.next()
            for k in range(NFC):
                MM(op_.t[:, :], slot.t[:, k * 128:(k + 1) * 128], gT[:, k, :], k == 0, k == NFC - 1,
                   [slot.buf, gTb[k]], [op_.buf])
            STT("dve", xT[:, blk, :], op_.t[:, :], 0.5, xT[:, blk, :], ALU.mult, ALU.add,
                [op_.buf, xTb[blk]], [xTb[blk]])

    def proj_fm(name, blk0, nblk, evac):
        for blk in range(nblk):
            slot = wload(name, blk0 + blk, 4096)
            for oc in range(4):
                p_ = bankrot.next()
                for k in range(8):
                    MM(p_.t[:, :], slot.t[:, k * 512 + oc * 128:k * 512 + oc * 128 + 128], hT[:, k, :], k == 0, k == 7,
                       [slot.buf, hTb[k]], [p_.buf])
                evac(blk * 4 + oc, p_)

    def conv_phase():
        rmsnorm(PG + 1 * 8)
        for blk in range(4):
            slot = wload("wpw1", blk, 4096)
            for fc in range(2):
                c = blk * 2 + fc
                vp = bankrot.next()
                gp = bankrot.next()
                for k in range(8):
                    MM(vp.t[:, :], slot.t[:, k * 512 + fc * 128:k * 512 + fc * 128 + 128], hT[:, k, :], k == 0, k == 7,
                       [slot.buf, hTb[k]], [vp.buf])
                for k in range(8):
                    MM(gp.t[:, :], slot.t[:, k * 512 + 256 + fc * 128:k * 512 + 256 + fc * 128 + 128], hT[:, k, :],
                       k == 0, k == 7, [slot.buf, hTb[k]], [gp.buf])
                sg = f32pool.next()
                ACT(sg.t[:, :], gp.t[:, :], AF.Sigmoid, [gp.buf, parb], [sg.buf], bias=pcol(PB1G + c))
                STT("dve", uT[:, c, HALO:HALO + T], vp.t[:, :], pcol(PB1V + c), sg.t[:, :], ALU.add, ALU.mult,
                    [vp.buf, sg.buf, parb], [uTb[c]])
        sumb = outrot.next()
        sqb = outrot.next()
        dgs = [None] * 8

        def build_diag(c):
            dg = diag.next()
            dgs[c] = dg
            for k in range(CW):
                TS("dve", dg.t[:, k, :], ident, pcol(PWDW + k * 8 + c), None, ALU.mult, None,
                   [cstb, parb], [dg.buf])
        build_diag(0)
        for c in range(8):
            if c + 1 < 8:
                build_diag(c + 1)
            dg = dgs[c]
            yp = bankrot.next()
            for k in range(CW):
                MM(yp.t[:, :], dg.t[:, k, :], uT[:, c, k:k + T], k == 0, k == CW - 1, [dg.buf, uTb[c]], [yp.buf])
            CP("dve", uT[:, c, 0:HALO], uT[:, c, T:T + HALO], [uTb[c]], [uTb[c]])
            ACT(yT[:, c, :], yp.t[:, :], AF.Identity, [yp.buf, parb], [yTb[c]], bias=pcol(PBDW + c))
            ysq = bfpool.next()
            ACT(ysq.t[:, :], yp.t[:, :], AF.Square, [yp.buf, parb], [ysq.buf], bias=pcol(PBDW + c))
            ybf = bfpool.next()
            CP("dve", ybf.t[:, :], yT[:, c, :], [yTb[c]], [ybf.buf])
            MM(sumb.t[:, :], ones, ybf.t[:, :], c == 0, c == 7, [ybf.buf, cstb], [sumb.buf])
            MM(sqb.t[:, :], ones, ysq.t[:, :], c == 0, c == 7, [ysq.buf, cstb], [sqb.buf])
        mean, msq, var, rstd = stat[0], stat[1], stat[2], stat[3]
        TS("dve", mean.t[:, :], sumb.t[:, :], 1.0 / D, None, ALU.mult, None, [sumb.buf], [mean.buf])
        TT("dve", msq.t[:, :], mean.t[:, :], mean.t[:, :], ALU.mult, [mean.buf], [msq.buf])
        STT("dve", var.t[:, :], sqb.t[:, :], 1.0 / D, msq.t[:, :], ALU.mult, ALU.subtract, [sqb.buf, msq.buf], [var.buf])
        ACT(rstd.t[:, :], var.t[:, :], AF.Sqrt, [var.buf], [rstd.buf], bias=LN_EPS)
        RECIP(rstd.t[:, :], rstd.t[:, :], [rstd.buf], [rstd.buf])
        for c in range(8):
            v1 = f32pool.next()
            TT("dve", v1.t[:, :], yT[:, c, :], mean.t[:, :], ALU.subtract, [yTb[c], mean.buf], [v1.buf])
            STT("dve", v1.t[:, :], v1.t[:, :], pcol(PLNG + c), rstd.t[:, :], ALU.mult, ALU.mult,
                [v1.buf, rstd.buf, parb], [v1.buf])
            ACT(hT[:, c, :], v1.t[:, :], AF.Silu, [v1.buf, parb], [hTb[c]], bias=pcol(PLNB + c))

        def ev(c, p_):
            STT("dve", xT[:, c, :], p_.t[:, :], pcol(PB2 + c), xT[:, c, :], ALU.add, ALU.add,
                [p_.buf, xTb[c], parb], [xTb[c]])
        proj_fm("wpw2", 0, 2, ev)

    def att_phase(ti):
        rmsnorm(PG + (3 + 1) * 8)

        def evq(c, p_):
            ACT(qT[:, c, :], p_.t[:, :], AF.Identity, [p_.buf], [qTb[c]], scale=0.125)

        def evk(c, p_):
            CP("dve", kst[:, c, :], p_.t[:, :], [p_.buf], [kstb])
        proj_fm("wqkv", 0, 2, evq)
        proj_fm("wqkv", 2, 2, evk)
        for blk in range(2):
            slot = wload("wqkv", 4 + blk, 4096)
            for tc in range(4):
                p_ = bankrot.next()
                for k in range(8):
                    MM(p_.t[:, :], hT[:, k, tc * 128:(tc + 1) * 128], slot.t[:, k * 512:(k + 1) * 512], k == 0, k == 7,
                       [slot.buf, hTb[k]], [p_.buf])
                CP("act" if tc % 2 else "dve", vst[:, tc, blk * 512:(blk + 1) * 512], p_.t[:, :], [p_.buf], [vstb])
        kcb = Buf()
        vcb = Buf()
        DMA("pool", kcd.rearrange("(c p) s -> p c s", p=128)[:, :, ti * T:(ti + 1) * T], kst[:, :, :], "kst",
            [kstb], [kcb])
        DMA("pool", vcd.rearrange("(t p) f -> p t f", p=128)[:, ti * 4:(ti + 1) * 4, :], vst[:, :, :], "vst",
            [vstb], [vcb])
        nk = (ti + 1) * T
        nch = nk // 128

        def kvload(p):
            ks, vs = kTp[p % 2], vP[p % 2]
            DMA("pool", ks.t[:, 0:nk], kcd[p * 128:(p + 1) * 128, 0:nk], ks.name, [kcb], [ks.buf])
            DMA("pool", vs.t[:, 0:nch, :], vcd.rearrange("(t p) f -> p t f", p=128)[:, 0:nch, p * 128:(p + 1) * 128],
                vs.name, [vcb], [vs.buf])
        kvload(0)
        kvload(1)
        NB = nch
        items = [(p, b) for p in range(8) for b in range(NB)]
        ctxs = {}

        def s_z(cx, sm, b):
            ch = nch - 1 - b
            zA = bankrot.next()
            sm["st"][b]["zA"] = zA
            diagb = b < 4
            MM(zA.t[:, :], cx["ks"].t[sm["pr"], ch * 128:(ch + 1) * 128], qT[sm["pr"], cx["p"], :], True, not diagb,
               [cx["ks"].buf, qTb[cx["p"]]], [zA.buf])
            if diagb:
                d = 3 - b
                MM(zA.t[:, :], ident, cst[:, CMASK + d * 512:CMASK + (d + 1) * 512], False, True, [cstb], [zA.buf])

        def s_act1(cx, sm, b):
            st = sm["st"][b]
            zA = st["zA"]
            e_ = epool.next()
            st["e"] = e_
            ACT(e_.t, zA.t[:, :], AF.Exp, [zA.buf], [e_.buf])
            sp = bfpool.next()
            st["sp"] = sp
            ACT(sp.t[:, :], e_.t, AF.Ln, [e_.buf], [sp.buf], bias=1.0)

        def s_tri(cx, sm, b):
            st = sm["st"][b]
            B = bankrot.next()
            st["B"] = B
            sp = st["sp"]
            MM(B.t[:, :], negU, sp.t[:, :], True, b == 0, [sp.buf, cstb], [B.buf])
            if b > 0:
                cr = sm["st"][b - 1]["cr"]
                MM(B.t[:, :], onesrow, cr.t[0:1, :], False, True, [cr.buf, cstb], [B.buf])

        def s_ext(cx, sm, b):
            st = sm["st"][b]
            if b < NB - 1:
                cr = carry.next()
                st["cr"] = cr
                CP("dve", cr.t[0:1, :], st["B"].t[0:1, :], [st["B"].buf], [cr.buf])

        def s_act2(cx, sm, b):
            st = sm["st"][b]
            x2 = f32pool.next()
            st["eb"] = x2
            extra = [st["cr"].buf] if "cr" in st else []
            ACT(x2.t[:, :], st["B"].t[:, :], AF.Exp, [st["B"].buf] + extra, [x2.buf])

        def s_a(cx, sm, b):
            st = sm["st"][b]
            a = bfpool.next()
            st["a"] = a
            TT("dve", a.t[:, :], st["e"].t, st["eb"].t[:, :], ALU.mult, [st["e"].buf, st["eb"].buf], [a.buf])

        def s_av(cx, sm, b):
            ch = nch - 1 - b
            a = sm["st"][b]["a"]
            MM(sm["ob"].t[:, :], cx["vs"].t[:, ch, :], a.t[:, :], b == 0, b == NB - 1, [cx["vs"].buf, a.buf],
               [sm["ob"].buf])

        NI = len(items)
        for g in range(NI + 3):
            if g < NI:
                p, b = items[g]
                if b == 0:
                    ctxs[p] = dict(p=p, ks=kTp[p % 2], vs=vP[p % 2],
                                   streams=[dict(pr=slice(e * 64, (e + 1) * 64), ob=outrot.next(),
                                                 st=[dict() for _ in range(NB)], e=e) for e in range(2)])
                cx = ctxs[p]
                for sm in cx["streams"]:
                    s_z(cx, sm, b)
                for sm in cx["streams"]:
                    s_act1(cx, sm, b)
            if 0 <= g - 1 < NI:
                p, b = items[g - 1]
                cx = ctxs[p]
                for sm in cx["streams"]:
                    s_tri(cx, sm, b)
                for sm in cx["streams"]:
                    s_ext(cx, sm, b)
                for sm in cx["streams"]:
                    s_act2(cx, sm, b)
                for sm in cx["streams"]:
                    s_a(cx, sm, b)
            if 0 <= g - 2 < NI:
                p, b = items[g - 2]
                cx = ctxs[p]
                for sm in cx["streams"]:
                    s_av(cx, sm, b)
                if b == NB - 1:
                    for sm in cx["streams"]:
                        CP("dve" if sm["e"] == 0 else "act", hT[sm["pr"], p, :], sm["ob"].t[sm["pr"], :],
                           [sm["ob"].buf], [hTb[p]])
                    del ctxs[p]
                    if p + 2 < 8:
                        kvload(p + 2)

        def evo(c, p_):
            TT("dve", xT[:, c, :], p_.t[:, :], xT[:, c, :], ALU.add, [p_.buf, xTb[c]], [xTb[c]])
        proj_fm("wao", 0, 2, evo)

    xTview = xTd.rearrange("(c p) s -> p c s", p=128)
    oTview = outd.rearrange("(c p) s -> p c s", p=128)
    outbufs = []
    for ti in range(ntiles):
        S_.seg = ti + 1
        cols = slice(ti * T, (ti + 1) * T)
        DMA("sp", xT[:, :, :], xTview[:, :, cols], "xld", [], xTb)
        ph = 0

        def more():
            nonlocal ph
            ph += 1
            return (ph in nphases) if isinstance(nphases, (set, frozenset)) else (ph <= nphases)
        if more():
            rmsnorm(PG + 0 * 8)
            ffn(0)
        if more():
            conv_phase()
            if debug and ti == 0:
                db = Buf()
                outbufs.append(db)
                DMA("pool", dbgd[0], uT[:, :, HALO:HALO + T], "dbg0", uTb, [db])
                DMA("pool", dbgd[1], yT[:, :, :], "dbg1", yTb, [db])
                DMA("pool", dbgd[2], hT[:, :, :], "dbg2", hTb, [db])
        if more():
            rmsnorm(PG + 2 * 8)
            ffn(1)
        if more():
            rmsnorm(PG + 3 * 8)
            ffn(2)
        if more():
            att_phase(ti)
        if more():
            rmsnorm(PG + 5 * 8)
            ffn(3)
        if more():
            ss = bankrot.next()
            for c in range(8):
                sq = bfpool.next()
                ACT(sq.t[:, :], xT[:, c, :], AF.Square, [xTb[c]], [sq.buf])
                MM(ss.t[:, :], ones, sq.t[:, :], c == 0, c == 7, [sq.buf, cstb], [ss.buf])
            rs = stat[0]
            ACT(rs.t[:, :], ss.t[:, :], AF.Sqrt, [ss.buf], [rs.buf], bias=D * RMS_EPS)
            RECIP(rs.t[:, :], rs.t[:, :], [rs.buf], [rs.buf])
            for c in range(8):
                o = osT.next()
                STT("dve", o.t[:, :], xT[:, c, :], pcol(PGF + c), rs.t[:, :], ALU.mult, ALU.mult,
                    [xTb[c], rs.buf, parb], [o.buf])
                ob_ = Buf()
                DMA("sp", outd[c * 128:(c + 1) * 128, cols], o.t[:, :], o.name, [o.buf], [ob_])
                outbufs.append(ob_)
        else:
            for c in range(8):
                ob_ = Buf()
                DMA("sp", outd[c * 128:(c + 1) * 128, cols], xT[:, c, :], f"xst{c}", [xTb[c]], [ob_])
                outbufs.append(ob_)
    S_.op("sp", None, outbufs, [])

    semnames = set()
    for eng in ENGS:
        cnt = {}
        for o in S_.ops[eng]:
            if o.dma is not None:
                semnames.add(("dma", o.dma[0]))
            elif o.sig:
                cnt[o.seg] = cnt.get(o.seg, 0) + 1
                o.val = cnt[o.seg]
                semnames.add((eng, o.seg))
    sems = {}
    for key in sorted(semnames, key=str):
        sems[key] = es.enter_context(nc.semaphore(f"s_{key[0]}_{key[1]}"))

    def event(d):
        if d.dma is not None:
            return sems[("dma", d.dma[0])], d.dma[1]
        return sems[(d.eng, d.seg)], d.val

    def emit(e, eng):
        waited = {}
        for o in S_.ops[eng]:
            for d in o.deps:
                sem, val = event(d)
                if waited.get(id(sem), 0) >= val:
                    continue
                e.wait_ge(sem, val)
                waited[id(sem)] = val
            if o.fn is None:
                continue
            ins = o.fn(e)
            if o.dma is not None:
                ins.then_inc(sems[("dma", o.dma[0])], 16)
            elif o.sig:
                ins.then_inc(sems[(eng, o.seg)], 1)

    with es:
        with nc.Block() as block:
            @block.tensor
            def _(e):
                emit(e, "pe")

            @block.scalar
            def _(e):
                emit(e, "act")

            @block.vector
            def _(e):
                emit(e, "dve")

            @block.gpsimd
            def _(e):
                emit(e, "pool")

            @block.sync
            def _(e):
                emit(e, "sp")
    return nc


def _tile_in(W, colsets):
    Wr = W.reshape(8, 128, W.shape[1])
    return np.concatenate(
        [np.ascontiguousarray(Wr[:, :, cols].transpose(1, 0, 2)).reshape(128, 8 * len(cols)) for cols in colsets], 0)


def _tile_out(W):
    Wr = W.reshape(NFC, 128, D)
    return np.concatenate(
        [np.ascontiguousarray(Wr[:, :, b * 128:(b + 1) * 128].transpose(1, 0, 2)).reshape(128, NFC * 128)
         for b in range(8)], 0)


def _consts():
    c = np.zeros((128, NCST), np.float32)
    c[:, CI:CI + 128] = np.eye(128, dtype=np.float32)
    c[:, CONES:CONES + 128] = 1.0
    j = np.arange(128)[:, None]
    s = np.arange(128)[None, :]
    c[:, CNEGU:CNEGU + 128] = np.where(j >= s, -1.0, 0.0)
    t = np.arange(512)[None, :]
    for d in range(4):
        c[:, CMASK + d * 512:CMASK + (d + 1) * 512] = np.where(128 * d + j >= t, -30000.0, 0.0)
    return c


def _prep(inputs):
    f = lambda a: np.asarray(a, dtype=np.float32)
    ffn_w_in, ffn_w_out = f(inputs["ffn_w_in"]), f(inputs["ffn_w_out"])
    r = np.arange(256)
    wi_sets = [np.concatenate([256 * b + r, DFF + 256 * b + r]) for b in range(11)]
    wi = np.concatenate([_tile_in(ffn_w_in[l, j], wi_sets) for l in range(2) for j in range(2)], 0)
    wo2 = np.concatenate([_tile_out(ffn_w_out[l, j]) for l in range(2) for j in range(2)], 0)
    pw1_sets = [np.concatenate([256 * b + r, D + 256 * b + r]) for b in range(4)]
    wpw1 = _tile_in(f(inputs["conv_w_pw1"])[0], pw1_sets)
    r5 = np.arange(512)
    wpw2 = _tile_in(f(inputs["conv_w_pw2"])[0], [512 * b + r5 for b in range(2)])
    wqkv = _tile_in(f(inputs["attn_w_qkv"])[0], [512 * b + r5 for b in range(6)])
    wao = _tile_in(f(inputs["attn_w_o"])[0], [512 * b + r5 for b in range(2)])
    rows = [f(inputs["norm_g"]).reshape(48, 128), f(inputs["final_g"]).reshape(8, 128),
            f(inputs["conv_b_pw1"]).reshape(16, 128), f(inputs["conv_b_dw"]).reshape(8, 128),
            f(inputs["conv_ln_g"]).reshape(8, 128), f(inputs["conv_ln_b"]).reshape(8, 128),
            f(inputs["conv_b_pw2"]).reshape(8, 128), f(inputs["conv_w_dw"]).reshape(CW * 8, 128)]
    par = np.ascontiguousarray(np.concatenate(rows, 0).T)
    assert par.shape == (128, NPAR)
    return dict(wi=wi, wo2=wo2, wpw1=wpw1, wpw2=wpw2, wqkv=wqkv, wao=wao, par=par, cst=_consts())


def kernel(**inputs):
    x = np.asarray(inputs["x"], dtype=np.float32)
    shared = _prep(inputs)
    nc = build()
    in_maps = []
    for b in range(NCORES):
        m = dict(shared)
        m["xT"] = np.ascontiguousarray(x[b].T)
        in_maps.append(m)
    res = run_bass_kernel_spmd(nc, in_maps, core_ids=list(range(NCORES)))
    out = np.stack([np.asarray(r["outT"]).T for r in res.results], 0)
    return np.ascontiguousarray(out.astype(np.float32))
```

```python
import numpy as np
from contextlib import ExitStack
import concourse.bass as bass
import concourse.mybir as mybir
from concourse.bass_utils import run_bass_kernel_spmd

F32 = mybir.dt.float32
BF16 = mybir.dt.bfloat16
AF = mybir.ActivationFunctionType
ALU = mybir.AluOpType

D = 1024
S = 4096
T = 512
NT = S // T
DFF = 2816
NFC = DFF // 128
CW = 31
HALO = CW - 1
NCORES = 8
RMS_EPS = 1e-6
LN_EPS = 1e-5

PG = 0
PGF = 48
PB1V = 56
PB1G = 64
PBDW = 72
PLNG = 80
PLNB = 88
PB2 = 96
PWDW = 104
NPAR = PWDW + CW * 8

CI = 0
CONES = 128
CNEGU = 256
CMASK = 384
NCST = CMASK + 4 * 512

ENGS = ("pe", "act", "dve", "pool", "sp")
import os as _os0
SAME_ENGINE_EDGES = _os0.environ.get("SEE", "1") == "1"


class Buf:
    __slots__ = ("w", "r")

    def __init__(self):
        self.w = None
        self.r = {}


class Op:
    __slots__ = ("eng", "fn", "deps", "sig", "seg", "val", "dma")


class Sched:
    def __init__(self):
        self.ops = {e: [] for e in ENGS}
        self.seg = 0
        self.dma_tot = {}

    def op(self, eng, fn, reads=(), writes=(), dma=None):
        o = Op()
        o.eng = eng
        o.fn = fn
        o.sig = False
        o.seg = self.seg
        o.val = None
        o.dma = None
        deps = []

        def add(d):
            if d is None or d in deps:
                return
            if d.dma is None and dma is None and d.eng == eng and (eng == "pe" or not SAME_ENGINE_EDGES):
                return
            if d.dma is None:
                d.sig = True
            deps.append(d)

        for b in reads:
            add(b.w)
        for b in writes:
            add(b.w)
            for r in b.r.values():
                add(r)
        o.deps = deps
        if dma is not None:
            tot = self.dma_tot.get(dma, 0) + 16
            self.dma_tot[dma] = tot
            o.dma = (dma, tot)
            key = ("dma", dma)
        else:
            key = eng
        for b in reads:
            b.r[key] = o
        for b in writes:
            b.w = o
            b.r = {}
        self.ops[eng].append(o)
        return o


class Slot:
    __slots__ = ("t", "buf", "name")

    def __init__(self, t, name=None):
        self.t = t
        self.buf = Buf()
        self.name = name


class Rot:
    def __init__(self, items):
        self.items = items
        self.i = 0

    def next(self):
        it = self.items[self.i % len(self.items)]
        self.i += 1
        return it


def build(nphases=99, ntiles=NT, debug=False):
    nc = bass.Bass("TRN2", target_bir_lowering=False)
    S_ = Sched()
    es = ExitStack()

    def dram(name, shape, dt, kind):
        return nc.dram_tensor(name, shape, dt, kind=kind).ap()

    xTd = dram("xT", [D, S], F32, "ExternalInput")
    pard = dram("par", [128, NPAR], F32, "ExternalInput")
    cstd = dram("cst", [128, NCST], F32, "ExternalInput")
    wspec = [("wi", 4 * 11 * 128, 4096), ("wo2", 4 * 8 * 128, 2816), ("wpw1", 4 * 128, 4096),
             ("wpw2", 2 * 128, 4096), ("wqkv", 6 * 128, 4096), ("wao", 2 * 128, 4096)]
    wf = {}
    wb = {}
    for name, rows, cols in wspec:
        wf[name] = dram(name, [rows, cols], F32, "ExternalInput")
        wb[name] = dram(name + "_bf", [rows, cols], BF16, "Internal")
    outd = dram("outT", [D, S], F32, "ExternalOutput")
    dbgd = dram("dbg", [3, 128, 8, T], F32, "ExternalOutput") if debug else None
    diagd = dram("diagm", [8, 128, CW * 128], BF16, "Internal")
    kcd = dram("kcache", [D, S], BF16, "Internal")
    vcd = dram("vcache", [S, D], BF16, "Internal")

    def sb(name, shape, dt):
        return es.enter_context(nc.sbuf_tensor(name, shape, dt))

    def pm(name):
        return es.enter_context(nc.psum_tensor(name, [128, 512], F32))

    xT = sb("xTs", [128, 8, T], F32)
    xTb = [Buf() for _ in range(8)]
    hT = sb("hT", [128, 8, T], BF16)
    hTb = [Buf() for _ in range(8)]
    G = sb("G", [128, 24, T], BF16)
    gT = G[:, 0:NFC, :]
    gTb = [Buf() for _ in range(NFC)]
    qT = G[:, 0:8, :]
    qTb = [Buf() for _ in range(8)]
    kst = G[:, 8:16, :]
    kstb = Buf()
    vst = G[:, 16:24, :].rearrange("p (a b) t -> p a (b t)", b=2)
    vstb = Buf()
    yT = sb("yT", [128, 8, T], F32)
    yTb = [Buf() for _ in range(8)]
    uT = sb("uT", [128, 8, HALO + T], BF16)
    uTb = [Buf() for _ in range(8)]
    ring = [Slot(sb(f"ring{i}", [128, 4096], BF16), f"ring{i}") for i in range(4)]
    ringrot = Rot(ring)
    kTp = [Slot(sb(f"kTp{i}", [128, S], BF16), f"kld{i}") for i in range(2)]
    vP = [Slot(sb(f"vP{i}", [128, S // 128, 128], BF16), f"vld{i}") for i in range(2)]
    f32pool = Rot([Slot(sb(f"f32p{i}", [128, 512], F32)) for i in range(8)])
    bfpool = Rot([Slot(sb(f"bfp{i}", [128, 512], BF16)) for i in range(8)])
    diag = Rot([Slot(sb(f"diag{i}", [128, CW, 128], BF16)) for i in range(2)])
    stat = [Slot(sb(f"stat{i}", [128, 512], F32)) for i in range(4)]
    carry = Rot([Slot(sb(f"carry{i}", [1, 512], BF16)) for i in range(4)])
    ones32 = sb("ones32", [1, 128], F32)
    o32b = Buf()
    par = sb("par_s", [128, NPAR], F32)
    parb = Buf()
    cst = sb("cst_s", [128, NCST], BF16)
    cstb = Buf()
    osT = Rot([Slot(sb(f"osT{i}", [128, 512], F32), f"out{i}") for i in range(2)])
    epool = Rot([Slot(yT[:, i, :]) for i in range(6)])

    banks = [Slot(pm(f"ps{i}")) for i in range(8)]
    bankrot = Rot(banks[0:6])
    outrot = Rot(banks[6:8])

    ident = cst[:, CI:CI + 128]
    ones = cst[:, CONES:CONES + 128]
    negU = cst[:, CNEGU:CNEGU + 128]
    onesrow = cst[0:1, CONES:CONES + 128]

    def pcol(col):
        return par[:, col:col + 1]

    def MM(out, lhsT, rhs, start, stop, reads, writes):
        S_.op("pe", lambda e: e.matmul(out, lhsT=lhsT, rhs=rhs, start=start, stop=stop), reads, writes)

    def ACT(out, in_, func, reads, writes, bias=None, scale=None):
        kw = {}
        if bias is not None:
            kw["bias"] = bias
        if scale is not None:
            kw["scale"] = scale
        S_.op("act", lambda e: e.activation(out=out, in_=in_, func=func, **kw), reads, writes)

    def TT(eng, out, in0, in1, op, reads, writes):
        S_.op(eng, lambda e: e.tensor_tensor(out=out, in0=in0, in1=in1, op=op), reads, writes)

    def TS(eng, out, in0, s1, s2, op0, op1, reads, writes):
        if s2 is None:
            S_.op(eng, lambda e: e.tensor_scalar(out=out, in0=in0, scalar1=s1, scalar2=None, op0=op0), reads, writes)
        else:
            S_.op(eng, lambda e: e.tensor_scalar(out=out, in0=in0, scalar1=s1, scalar2=s2, op0=op0, op1=op1), reads, writes)

    def STT(eng, out, in0, scalar, in1, op0, op1, reads, writes):
        S_.op(eng, lambda e: e.scalar_tensor_tensor(out=out, in0=in0, scalar=scalar, in1=in1, op0=op0, op1=op1),
              reads, writes)

    def CP(eng, out, in_, reads, writes):
        if eng == "act":
            S_.op("act", lambda e: e.activation(out=out, in_=in_, func=AF.Copy), reads, writes)
        else:
            S_.op(eng, lambda e: e.tensor_copy(out=out, in_=in_), reads, writes)

    def RECIP(out, in_, reads, writes):
        S_.op("dve", lambda e: e.reciprocal(out=out, in_=in_), reads, writes)

    def DMA(eng, out, in_, sem, reads, writes):
        S_.op(eng, lambda e: e.dma_start(out=out, in_=in_), reads, writes, dma=sem)

    DMA("sp", par[:, :], pard[:, :], "par", [], [parb])
    DMA("pool", cst[:, :], cstd[:, :], "cst", [], [cstb])
    TS("dve", par[:, 0:PB1V], par[:, 0:PB1V], 32.0, None, ALU.mult, None, [parb], [parb])
    for c in range(8):
        S_.op("dve", (lambda c: lambda e: e.memset(uT[:, c, 0:HALO], 0.0))(c), [], [uTb[c]])
    S_.op("dve", lambda e: e.memset(ones32[0:1, :], 1.0), [], [o32b])

    wbuf = {}
    conv_order = [("wi", 0, 11), ("wo2", 0, 8), ("wpw1", 0, 4), ("wpw2", 0, 2), ("wi", 11, 11), ("wo2", 8, 8),
                  ("wi", 22, 11), ("wo2", 16, 8), ("wqkv", 0, 6), ("wao", 0, 2), ("wi", 33, 11), ("wo2", 24, 8)]
    ncv = 0
    for name, b0, nb in conv_order:
        b = b0
        while b < b0 + nb:
            n = min(4, b0 + nb - b)
            cb = Buf()
            DMA("pool", wb[name][b * 128:(b + n) * 128, :], wf[name][b * 128:(b + n) * 128, :], f"cv{ncv}", [], [cb])
            ncv += 1
            for bb in range(b, b + n):
                wbuf[(name, bb)] = cb
            b += n

    diagb = []
    for c in range(8):
        dg = diag.next()
        for k in range(CW):
            TS("dve", dg.t[:, k, :], ident, pcol(PWDW + k * 8 + c), None, ALU.mult, None, [cstb, parb], [dg.buf])
        db_ = Buf()
        DMA("pool", diagd[c], dg.t[:, :, :].rearrange("p k m -> p (k m)"), f"dgs{c % 2}", [dg.buf], [db_])
        diagb.append(db_)

    def wload(name, blk, ncols):
        slot = ringrot.next()
        DMA("sp", slot.t[:, 0:ncols], wb[name][blk * 128:(blk + 1) * 128, 0:ncols], slot.name,
            [wbuf[(name, blk)]], [slot.buf])
        return slot

    def rmsnorm(gcol, out_f32=None):
        ss = bankrot.next()
        for c in range(8):
            sq = bfpool.next()
            ACT(sq.t[:, :], xT[:, c, :], AF.Square, [xTb[c]], [sq.buf])
            MM(ss.t[:, :], ones, sq.t[:, :], c == 0, c == 7, [sq.buf, cstb], [ss.buf])
        rs = stat[0]
        ACT(rs.t[:, :], ss.t[:, :], AF.Sqrt, [ss.buf], [rs.buf], bias=D * RMS_EPS)
        RECIP(rs.t[:, :], rs.t[:, :], [rs.buf], [rs.buf])
        for c in range(8):
            if False:
                tmp = f32pool.next()
                TS("pool", tmp.t[:, :], xT[:, c, :], pcol(gcol + c), None, ALU.mult, None, [xTb[c], parb], [tmp.buf])
                TT("pool", hT[:, c, :], tmp.t[:, :], rs.t[:, :], ALU.mult, [tmp.buf, rs.buf], [hTb[c]])
            elif out_f32 is None:
                STT("dve", hT[:, c, :], xT[:, c, :], pcol(gcol + c), rs.t[:, :],
                    ALU.mult, ALU.mult, [xTb[c], rs.buf, parb], [hTb[c]])
            else:
                o = out_f32[c]
                STT("dve", o.t[:, :], xT[:, c, :], pcol(gcol + c), rs.t[:, :], ALU.mult, ALU.mult,
                    [xTb[c], rs.buf, parb], [o.buf])

    def ffn(lj):
        for blk in range(11):
            slot = wload("wi", lj * 11 + blk, 4096)
            for fc in range(2):
                f = blk * 2 + fc
                gp = bankrot.next()
                up = bankrot.next()
                for k in range(8):
                    MM(gp.t[:, :], slot.t[:, k * 512 + fc * 128:k * 512 + fc * 128 + 128], hT[:, k, :], k == 0, k == 7,
                       [slot.buf, hTb[k]], [gp.buf])
                for k in range(8):
                    MM(up.t[:, :], slot.t[:, k * 512 + 256 + fc * 128:k * 512 + 256 + fc * 128 + 128], hT[:, k, :],
                       k == 0, k == 7, [slot.buf, hTb[k]], [up.buf])
                sg = f32pool.next()
                ACT(sg.t[:, :], gp.t[:, :], AF.Silu, [gp.buf], [sg.buf])
                TT("dve", gT[:, f, :], sg.t[:, :], up.t[:, :], ALU.mult, [sg.buf, up.buf], [gTb[f]])
        for blk in range(8):
            slot = wload("wo2", lj * 8 + blk, 2816)
            op_ = bankrot.next()
            for k in range(NFC):
                MM(op_.t[:, :], slot.t[:, k * 128:(k + 1) * 128], gT[:, k, :], k == 0, k == NFC - 1,
                   [slot.buf, gTb[k]], [op_.buf])
            STT("dve", xT[:, blk, :], op_.t[:, :], 0.5, xT[:, blk, :], ALU.mult, ALU.add,
                [op_.buf, xTb[blk]], [xTb[blk]])

    def proj_fm(name, blk0, nblk, evac):
        for blk in range(nblk):
            slot = wload(name, blk0 + blk, 4096)
            for oc in range(4):
                p_ = bankrot.next()
                for k in range(8):
                    MM(p_.t[:, :], slot.t[:, k * 512 + oc * 128:k * 512 + oc * 128 + 128], hT[:, k, :], k == 0, k == 7,
                       [slot.buf, hTb[k]], [p_.buf])
                evac(blk * 4 + oc, p_)

    def conv_phase():
        rmsnorm(PG + 1 * 8)
        for blk in range(4):
            slot = wload("wpw1", blk, 4096)
            for fc in range(2):
                c = blk * 2 + fc
                vp = bankrot.next()
                gp = bankrot.next()
                for k in range(8):
                    MM(vp.t[:, :], slot.t[:, k * 512 + fc * 128:k * 512 + fc * 128 + 128], hT[:, k, :], k == 0, k == 7,
                       [slot.buf, hTb[k]], [vp.buf])
                for k in range(8):
                    MM(gp.t[:, :], slot.t[:, k * 512 + 256 + fc * 128:k * 512 + 256 + fc * 128 + 128], hT[:, k, :],
                       k == 0, k == 7, [slot.buf, hTb[k]], [gp.buf])
                sg = f32pool.next()
                ACT(sg.t[:, :], gp.t[:, :], AF.Sigmoid, [gp.buf, parb], [sg.buf], bias=pcol(PB1G + c))
                STT("dve", uT[:, c, HALO:HALO + T], vp.t[:, :], pcol(PB1V + c), sg.t[:, :], ALU.add, ALU.mult,
                    [vp.buf, sg.buf, parb], [uTb[c]])
        sumb = outrot.next()
        sqb = outrot.next()
        dgs = [None] * 8

        def build_diag(c):
            dg = diag.next()
            dgs[c] = dg
            DMA("pool", dg.t[:, :, :].rearrange("p k m -> p (k m)"), diagd[c], f"dgl{c % 2}", [diagb[c]], [dg.buf])
        build_diag(0)
        for c in range(8):
            if c + 1 < 8:
                build_diag(c + 1)
            dg = dgs[c]
            yp = bankrot.next()
            for k in range(CW):
                MM(yp.t[:, :], dg.t[:, k, :], uT[:, c, k:k + T], k == 0, k == CW - 1, [dg.buf, uTb[c]], [yp.buf])
            CP("dve", uT[:, c, 0:HALO], uT[:, c, T:T + HALO], [uTb[c]], [uTb[c]])
            ACT(yT[:, c, :], yp.t[:, :], AF.Identity, [yp.buf, parb], [yTb[c]], bias=pcol(PBDW + c))
            ysq = bfpool.next()
            ACT(ysq.t[:, :], yp.t[:, :], AF.Square, [yp.buf, parb], [ysq.buf], bias=pcol(PBDW + c))
            ybf = bfpool.next()
            CP("dve", ybf.t[:, :], yT[:, c, :], [yTb[c]], [ybf.buf])
            MM(sumb.t[:, :], ones, ybf.t[:, :], c == 0, c == 7, [ybf.buf, cstb], [sumb.buf])
            MM(sqb.t[:, :], ones, ysq.t[:, :], c == 0, c == 7, [ysq.buf, cstb], [sqb.buf])
        mean, msq, var, rstd = stat[0], stat[1], stat[2], stat[3]
        TS("dve", mean.t[:, :], sumb.t[:, :], 1.0 / D, None, ALU.mult, None, [sumb.buf], [mean.buf])
        TT("dve", msq.t[:, :], mean.t[:, :], mean.t[:, :], ALU.mult, [mean.buf], [msq.buf])
        STT("dve", var.t[:, :], sqb.t[:, :], 1.0 / D, msq.t[:, :], ALU.mult, ALU.subtract, [sqb.buf, msq.buf], [var.buf])
        ACT(rstd.t[:, :], var.t[:, :], AF.Sqrt, [var.buf], [rstd.buf], bias=LN_EPS)
        RECIP(rstd.t[:, :], rstd.t[:, :], [rstd.buf], [rstd.buf])
        for c in range(8):
            v1 = f32pool.next()
            TT("dve", v1.t[:, :], yT[:, c, :], mean.t[:, :], ALU.subtract, [yTb[c], mean.buf], [v1.buf])
            STT("dve", v1.t[:, :], v1.t[:, :], pcol(PLNG + c), rstd.t[:, :], ALU.mult, ALU.mult,
                [v1.buf, rstd.buf, parb], [v1.buf])
            ACT(hT[:, c, :], v1.t[:, :], AF.Silu, [v1.buf, parb], [hTb[c]], bias=pcol(PLNB + c))

        def ev(c, p_):
            STT("dve", xT[:, c, :], p_.t[:, :], pcol(PB2 + c), xT[:, c, :], ALU.add, ALU.add,
                [p_.buf, xTb[c], parb], [xTb[c]])
        proj_fm("wpw2", 0, 2, ev)

    def att_phase(ti):
        rmsnorm(PG + (3 + 1) * 8)

        def evq(c, p_):
            ACT(qT[:, c, :], p_.t[:, :], AF.Identity, [p_.buf], [qTb[c]], scale=0.125)

        def evk(c, p_):
            CP("dve", kst[:, c, :], p_.t[:, :], [p_.buf], [kstb])
        proj_fm("wqkv", 0, 2, evq)
        proj_fm("wqkv", 2, 2, evk)
        for blk in range(2):
            slot = wload("wqkv", 4 + blk, 4096)
            for tc in range(4):
                p_ = bankrot.next()
                for k in range(8):
                    MM(p_.t[:, :], hT[:, k, tc * 128:(tc + 1) * 128], slot.t[:, k * 512:(k + 1) * 512], k == 0, k == 7,
                       [slot.buf, hTb[k]], [p_.buf])
                CP("act" if tc % 2 else "dve", vst[:, tc, blk * 512:(blk + 1) * 512], p_.t[:, :], [p_.buf], [vstb])
        kcb = Buf()
        vcb = Buf()
        DMA("pool", kcd.rearrange("(c p) s -> p c s", p=128)[:, :, ti * T:(ti + 1) * T], kst[:, :, :], "kst",
            [kstb], [kcb])
        DMA("pool", vcd.rearrange("(t p) f -> p t f", p=128)[:, ti * 4:(ti + 1) * 4, :], vst[:, :, :], "vst",
            [vstb], [vcb])
        nk = (ti + 1) * T
        nch = nk // 128

        def kvload(p):
            ks, vs = kTp[p % 2], vP[p % 2]
            DMA("pool", ks.t[:, 0:nk], kcd[p * 128:(p + 1) * 128, 0:nk], ks.name, [kcb], [ks.buf])
            DMA("pool", vs.t[:, 0:nch, :], vcd.rearrange("(t p) f -> p t f", p=128)[:, 0:nch, p * 128:(p + 1) * 128],
                vs.name, [vcb], [vs.buf])
        kvload(0)
        kvload(1)
        NB = nch
        items = [(p, b) for p in range(8) for b in range(NB)]
        ctxs = {}

        def s_z(cx, sm, b):
            ch = nch - 1 - b
            zA = bankrot.next()
            sm["st"][b]["zA"] = zA
            diagb = b < 4
            MM(zA.t[:, :], cx["ks"].t[sm["pr"], ch * 128:(ch + 1) * 128], qT[sm["pr"], cx["p"], :], True, not diagb,
               [cx["ks"].buf, qTb[cx["p"]]], [zA.buf])
            if diagb:
                d = 3 - b
                MM(zA.t[:, :], ident, cst[:, CMASK + d * 512:CMASK + (d + 1) * 512], False, True, [cstb], [zA.buf])

        def s_act1(cx, sm, b):
            st = sm["st"][b]
            zA = st["zA"]
            e_ = epool.next()
            st["e"] = e_
            ACT(e_.t, zA.t[:, :], AF.Exp, [zA.buf], [e_.buf])
            sp = bfpool.next()
            st["sp"] = sp
            ACT(sp.t[:, :], e_.t, AF.Ln, [e_.buf], [sp.buf], bias=1.0)

        def s_tri(cx, sm, b):
            st = sm["st"][b]
            B = bankrot.next()
            st["B"] = B
            sp = st["sp"]
            MM(B.t[:, :], negU, sp.t[:, :], True, b == 0, [sp.buf, cstb], [B.buf])
            if b > 0:
                cr = sm["st"][b - 1]["cr"]
                MM(B.t[:, :], onesrow, cr.t[0:1, :], False, True, [cr.buf, cstb], [B.buf])

        def s_ext(cx, sm, b):
            st = sm["st"][b]
            if b < NB - 1:
                cr = carry.next()
                st["cr"] = cr
                CP("dve", cr.t[0:1, :], st["B"].t[0:1, :], [st["B"].buf], [cr.buf])

        def s_act2(cx, sm, b):
            st = sm["st"][b]
            x2 = f32pool.next()
            st["eb"] = x2
            extra = [st["cr"].buf] if "cr" in st else []
            ACT(x2.t[:, :], st["B"].t[:, :], AF.Exp, [st["B"].buf] + extra, [x2.buf])

        def s_a(cx, sm, b):
            st = sm["st"][b]
            a = bfpool.next()
            st["a"] = a
            TT("dve", a.t[:, :], st["e"].t, st["eb"].t[:, :], ALU.mult, [st["e"].buf, st["eb"].buf], [a.buf])

        def s_av(cx, sm, b):
            ch = nch - 1 - b
            a = sm["st"][b]["a"]
            MM(sm["ob"].t[:, :], cx["vs"].t[:, ch, :], a.t[:, :], b == 0, b == NB - 1, [cx["vs"].buf, a.buf],
               [sm["ob"].buf])

        NI = len(items)
        for g in range(NI + 3):
            if g < NI:
                p, b = items[g]
                if b == 0:
                    ctxs[p] = dict(p=p, ks=kTp[p % 2], vs=vP[p % 2],
                                   streams=[dict(pr=slice(e * 64, (e + 1) * 64), ob=outrot.next(),
                                                 st=[dict() for _ in range(NB)], e=e) for e in range(2)])
                cx = ctxs[p]
                for sm in cx["streams"]:
                    s_z(cx, sm, b)
                for sm in cx["streams"]:
                    s_act1(cx, sm, b)
            if 0 <= g - 1 < NI:
                p, b = items[g - 1]
                cx = ctxs[p]
                for sm in cx["streams"]:
                    s_tri(cx, sm, b)
                for sm in cx["streams"]:
                    s_ext(cx, sm, b)
                for sm in cx["streams"]:
                    s_act2(cx, sm, b)
                for sm in cx["streams"]:
                    s_a(cx, sm, b)
            if 0 <= g - 2 < NI:
                p, b = items[g - 2]
                cx = ctxs[p]
                for sm in cx["streams"]:
                    s_av(cx, sm, b)
                if b == NB - 1:
                    for sm in cx["streams"]:
                        CP("dve" if sm["e"] == 0 else "act", hT[sm["pr"], p, :], sm["ob"].t[sm["pr"], :],
                           [sm["ob"].buf], [hTb[p]])
                    del ctxs[p]
                    if p + 2 < 8:
                        kvload(p + 2)

        def evo(c, p_):
            TT("dve", xT[:, c, :], p_.t[:, :], xT[:, c, :], ALU.add, [p_.buf, xTb[c]], [xTb[c]])
        proj_fm("wao", 0, 2, evo)

    xTview = xTd.rearrange("(c p) s -> p c s", p=128)
    oTview = outd.rearrange("(c p) s -> p c s", p=128)
    outbufs = []
    for ti in range(ntiles):
        S_.seg = ti + 1
        cols = slice(ti * T, (ti + 1) * T)
        DMA("sp", xT[:, :, :], xTview[:, :, cols], "xld", [], xTb)
        ph = 0

        def more():
            nonlocal ph
            ph += 1
            return (ph in nphases) if isinstance(nphases, (set, frozenset)) else (ph <= nphases)
        if more():
            rmsnorm(PG + 0 * 8)
            ffn(0)
        if more():
            conv_phase()
            if debug and ti == 0:
                db = Buf()
                outbufs.append(db)
                DMA("pool", dbgd[0], uT[:, :, HALO:HALO + T], "dbg0", uTb, [db])
                DMA("pool", dbgd[1], yT[:, :, :], "dbg1", yTb, [db])
                DMA("pool", dbgd[2], hT[:, :, :], "dbg2", hTb, [db])
        if more():
            rmsnorm(PG + 2 * 8)
            ffn(1)
        if more():
            rmsnorm(PG + 3 * 8)
            ffn(2)
        if more():
            att_phase(ti)
        if more():
            rmsnorm(PG + 5 * 8)
            ffn(3)
        if more():
            ss = bankrot.next()
            for c in range(8):
                sq = bfpool.next()
                ACT(sq.t[:, :], xT[:, c, :], AF.Square, [xTb[c]], [sq.buf])
                MM(ss.t[:, :], ones, sq.t[:, :], c == 0, c == 7, [sq.buf, cstb], [ss.buf])
            rs = stat[0]
            ACT(rs.t[:, :], ss.t[:, :], AF.Sqrt, [ss.buf], [rs.buf], bias=D * RMS_EPS)
            RECIP(rs.t[:, :], rs.t[:, :], [rs.buf], [rs.buf])
            for c in range(8):
                o = osT.next()
                STT("dve", o.t[:, :], xT[:, c, :], pcol(PGF + c), rs.t[:, :], ALU.mult, ALU.mult,
                    [xTb[c], rs.buf, parb], [o.buf])
                ob_ = Buf()
                DMA("sp", outd[c * 128:(c + 1) * 128, cols], o.t[:, :], o.name, [o.buf], [ob_])
                outbufs.append(ob_)
        else:
            for c in range(8):
                ob_ = Buf()
                DMA("sp", outd[c * 128:(c + 1) * 128, cols], xT[:, c, :], f"xst{c}", [xTb[c]], [ob_])
                outbufs.append(ob_)
    S_.op("sp", None, outbufs, [])

    semnames = set()
    for eng in ENGS:
        cnt = {}
        for o in S_.ops[eng]:
            if o.dma is not None:
                semnames.add(("dma", o.dma[0]))
            elif o.sig:
                cnt[o.seg] = cnt.get(o.seg, 0) + 1
                o.val = cnt[o.seg]
                semnames.add((eng, o.seg))
    sems = {}
    for key in sorted(semnames, key=str):
        sems[key] = es.enter_context(nc.semaphore(f"s_{key[0]}_{key[1]}"))

    def event(d):
        if d.dma is not None:
            return sems[("dma", d.dma[0])], d.dma[1]
        return sems[(d.eng, d.seg)], d.val

    def emit(e, eng):
        waited = {}
        for o in S_.ops[eng]:
            for d in o.deps:
                sem, val = event(d)
                if waited.get(id(sem), 0) >= val:
                    continue
                e.wait_ge(sem, val)
                waited[id(sem)] = val
            if o.fn is None:
                continue
            ins = o.fn(e)
            if o.dma is not None:
                ins.then_inc(sems[("dma", o.dma[0])], 16)
            elif o.sig:
                ins.then_inc(sems[(eng, o.seg)], 1)

    with es:
        with nc.Block() as block:
            @block.tensor
            def _(e):
                emit(e, "pe")

            @block.scalar
            def _(e):
                emit(e, "act")

            @block.vector
            def _(e):
                emit(e, "dve")

            @block.gpsimd
            def _(e):
                emit(e, "pool")

            @block.sync
            def _(e):
                emit(e, "sp")
    return nc


def _tile_in(W, colsets):
    Wr = W.reshape(8, 128, W.shape[1])
    return np.concatenate(
        [np.ascontiguousarray(Wr[:, :, cols].transpose(1, 0, 2)).reshape(128, 8 * len(cols)) for cols in colsets], 0)


def _tile_out(W):
    Wr = W.reshape(NFC, 128, D)
    return np.concatenate(
        [np.ascontiguousarray(Wr[:, :, b * 128:(b + 1) * 128].transpose(1, 0, 2)).reshape(128, NFC * 128)
         for b in range(8)], 0)


def _consts():
    c = np.zeros((128, NCST), np.float32)
    c[:, CI:CI + 128] = np.eye(128, dtype=np.float32)
    c[:, CONES:CONES + 128] = 1.0
    j = np.arange(128)[:, None]
    s = np.arange(128)[None, :]
    c[:, CNEGU:CNEGU + 128] = np.where(j >= s, -1.0, 0.0)
    t = np.arange(512)[None, :]
    for d in range(4):
        c[:, CMASK + d * 512:CMASK + (d + 1) * 512] = np.where(128 * d + j >= t, -30000.0, 0.0)
    return c


def _prep(inputs):
    f = lambda a: np.asarray(a, dtype=np.float32)
    ffn_w_in, ffn_w_out = f(inputs["ffn_w_in"]), f(inputs["ffn_w_out"])
    r = np.arange(256)
    wi_sets = [np.concatenate([256 * b + r, DFF + 256 * b + r]) for b in range(11)]
    wi = np.concatenate([_tile_in(ffn_w_in[l, j], wi_sets) for l in range(2) for j in range(2)], 0)
    wo2 = np.concatenate([_tile_out(ffn_w_out[l, j]) for l in range(2) for j in range(2)], 0)
    pw1_sets = [np.concatenate([256 * b + r, D + 256 * b + r]) for b in range(4)]
    wpw1 = _tile_in(f(inputs["conv_w_pw1"])[0], pw1_sets)
    r5 = np.arange(512)
    wpw2 = _tile_in(f(inputs["conv_w_pw2"])[0], [512 * b + r5 for b in range(2)])
    wqkv = _tile_in(f(inputs["attn_w_qkv"])[0], [512 * b + r5 for b in range(6)])
    wao = _tile_in(f(inputs["attn_w_o"])[0], [512 * b + r5 for b in range(2)])
    rows = [f(inputs["norm_g"]).reshape(48, 128), f(inputs["final_g"]).reshape(8, 128),
            f(inputs["conv_b_pw1"]).reshape(16, 128), f(inputs["conv_b_dw"]).reshape(8, 128),
            f(inputs["conv_ln_g"]).reshape(8, 128), f(inputs["conv_ln_b"]).reshape(8, 128),
            f(inputs["conv_b_pw2"]).reshape(8, 128), f(inputs["conv_w_dw"]).reshape(CW * 8, 128)]
    par = np.ascontiguousarray(np.concatenate(rows, 0).T)
    assert par.shape == (128, NPAR)
    return dict(wi=wi, wo2=wo2, wpw1=wpw1, wpw2=wpw2, wqkv=wqkv, wao=wao, par=par, cst=_consts())


def kernel(**inputs):
    x = np.asarray(inputs["x"], dtype=np.float32)
    shared = _prep(inputs)
    nc = build()
    in_maps = []
    for b in range(NCORES):
        m = dict(shared)
        m["xT"] = np.ascontiguousarray(x[b].T)
        in_maps.append(m)
    res = run_bass_kernel_spmd(nc, in_maps, core_ids=list(range(NCORES)))
    out = np.stack([np.asarray(r["outT"]).T for r in res.results], 0)
    return np.ascontiguousarray(out.astype(np.float32))
```
